# Optimizing a Trainium2 kernel written in Bass

```python
import jax, jax.numpy as jnp
from jax import lax
import numpy as np

D_MODEL = 1024
BATCH = 32
SEQ = 2048
DEPTH = 1

HEAD_DIM = 64
CONV_WIDTH = 3
D_CONV = D_MODEL // 2
POOL_WINDOWS = (2, 4, 8, 16)
N_POOL_GROUPS = len(POOL_WINDOWS)
POOL_GROUP_DIM = D_MODEL // 16
D_POOL = N_POOL_GROUPS * POOL_GROUP_DIM
N_XATTN_HEADS = 4
D_XATTN = N_XATTN_HEADS * HEAD_DIM
MEM_LEN = 256
D_IN = 3 * D_CONV + D_POOL + D_XATTN
N_BRANCHES = 3
D_FF = 11 * D_MODEL // 4
EPS = 1e-6

kernel_name = "hybrid_conv_pool_xattn_block"


def rmsnorm(x, g):
    xf = x.astype(jnp.float32)
    y = xf * lax.rsqrt(jnp.mean(xf * xf, axis=-1, keepdims=True) + EPS)
    return (y * g.astype(jnp.float32)).astype(x.dtype)


def causal_dwconv(u, w):
    k, c = w.shape
    return lax.conv_general_dilated(
        u, w[:, None, :].astype(u.dtype), window_strides=(1,),
        padding=[(k - 1, 0)], dimension_numbers=("NWC", "WIO", "NWC"),
        feature_group_count=c)


def multiscale_pool(u, pool_w, pool_scale):
    b, s, _ = u.shape
    uf = u.astype(jnp.float32).reshape(b, s, N_POOL_GROUPS, POOL_GROUP_DIM)
    cs = jnp.pad(jnp.cumsum(uf, axis=1), ((0, 0), (1, 0), (0, 0), (0, 0)))
    pos = jnp.arange(1, s + 1, dtype=jnp.float32)
    outs = []
    for gi, w in enumerate(POOL_WINDOWS):
        csg = cs[:, :, gi]
        lag = jnp.pad(csg[:, : s + 1 - w], ((0, 0), (w, 0), (0, 0)))
        count = jnp.minimum(jnp.float32(w), pos)[None, :, None]
        outs.append((csg[:, 1:] - lag[:, 1:]) / count - uf[:, :, gi])
    pooled = jnp.stack(outs, axis=2)
    mixed = jnp.einsum("bsgc,gcd->bsgd", pooled, pool_w.astype(jnp.float32))
    y = mixed.reshape(b, s, D_POOL) * pool_scale.astype(jnp.float32)
    return y.astype(u.dtype)


def memory_cross_attention(q, mem, g_mem, w_kv):
    b, s, _ = q.shape
    m = mem.shape[1]
    kv = rmsnorm(mem, g_mem) @ w_kv
    k, v = jnp.split(kv, 2, axis=-1)
    qh = q.reshape(b, s, N_XATTN_HEADS, HEAD_DIM)
    kh = k.reshape(b, m, N_XATTN_HEADS, HEAD_DIM)
    vh = v.reshape(b, m, N_XATTN_HEADS, HEAD_DIM)
    scores = jnp.einsum("bshd,bmhd->bhsm", qh, kh).astype(jnp.float32) * (HEAD_DIM ** -0.5)
    p = jax.nn.softmax(scores, axis=-1).astype(vh.dtype)
    o = jnp.einsum("bhsm,bmhd->bshd", p, vh)
    return o.reshape(b, s, D_XATTN)


def setup_inputs(seed: int = 0) -> dict:
    key = jax.random.key(seed)
    ks = jax.random.split(key, 24)

    def nrm(k, shape, fan_in):
        return jax.random.normal(k, shape, jnp.float32) * (fan_in ** -0.5)

    def gain(k, shape):
        return 1.0 + 0.05 * jax.random.normal(k, shape, jnp.float32)

    L = DEPTH
    return {
        "x": jax.random.normal(ks[0], (BATCH, SEQ, D_MODEL), jnp.float32),
        "mem": jax.random.normal(ks[1], (BATCH, MEM_LEN, D_MODEL), jnp.float32),
        "g_mix_pre": gain(ks[2], (L, D_MODEL)),
        "w_in": nrm(ks[3], (L, D_MODEL, D_IN), D_MODEL),
        "conv_w": nrm(ks[4], (L, CONV_WIDTH, D_CONV), CONV_WIDTH),
        "pool_w": nrm(ks[5], (L, N_POOL_GROUPS, POOL_GROUP_DIM, POOL_GROUP_DIM), POOL_GROUP_DIM),
        "pool_scale": gain(ks[6], (L, D_POOL)),
        "g_mem": gain(ks[7], (L, D_MODEL)),
        "w_kv": nrm(ks[8], (L, D_MODEL, 2 * D_XATTN), D_MODEL),
        "w_br_conv": nrm(ks[9], (L, D_CONV, D_MODEL), D_CONV),
        "w_br_pool": nrm(ks[10], (L, D_POOL, D_MODEL), D_POOL),
        "w_br_attn": nrm(ks[11], (L, D_XATTN, D_MODEL), D_XATTN),
        "w_gate": nrm(ks[12], (L, D_MODEL, N_BRANCHES * D_MODEL), D_MODEL),
        "b_gate": 0.01 * jax.random.normal(ks[13], (L, N_BRANCHES * D_MODEL), jnp.float32),
        "w_o": nrm(ks[14], (L, D_MODEL, D_MODEL), D_MODEL),
        "g_mix_post": gain(ks[15], (L, D_MODEL)),
        "g_ffn_pre": gain(ks[16], (L, D_MODEL)),
        "w_up": nrm(ks[17], (L, D_MODEL, 2 * D_FF), D_MODEL),
        "ffn_conv_w": nrm(ks[18], (L, CONV_WIDTH, 2 * D_FF), CONV_WIDTH),
        "w_down": nrm(ks[19], (L, D_FF, D_MODEL), D_FF),
        "g_ffn_post": gain(ks[20], (L, D_MODEL)),
    }


def reference(x, mem, g_mix_pre, w_in, conv_w, pool_w, pool_scale, g_mem, w_kv,
              w_br_conv, w_br_pool, w_br_attn, w_gate, b_gate, w_o, g_mix_post,
              g_ffn_pre, w_up, ffn_conv_w, w_down, g_ffn_post):
    b, s, d = x.shape
    splits = (D_CONV, 2 * D_CONV, 3 * D_CONV, 3 * D_CONV + D_POOL)
    for l in range(DEPTH):
        h = rmsnorm(x, g_mix_pre[l])
        proj = h @ w_in[l]
        b_c, c_c, v_c, u_pool, q = jnp.split(proj, splits, axis=-1)
        y_conv = b_c * causal_dwconv(c_c * v_c, conv_w[l])
        y_pool = multiscale_pool(u_pool, pool_w[l], pool_scale[l])
        y_attn = memory_cross_attention(q, mem, g_mem[l], w_kv[l])
        gates = jax.nn.sigmoid(h @ w_gate[l] + b_gate[l]).reshape(b, s, N_BRANCHES, d)
        merged = (gates[:, :, 0] * (y_conv @ w_br_conv[l])
                  + gates[:, :, 1] * (y_pool @ w_br_pool[l])
                  + gates[:, :, 2] * (y_attn @ w_br_attn[l]))
        x = x + rmsnorm(merged @ w_o[l], g_mix_post[l])
        h = rmsnorm(x, g_ffn_pre[l])
        up = causal_dwconv(h @ w_up[l], ffn_conv_w[l])
        gate, val = jnp.split(up, 2, axis=-1)
        ff = (jax.nn.gelu(gate, approximate=True) * val) @ w_down[l]
        x = x + rmsnorm(ff, g_ffn_post[l])
    return x
```

```python
import math
from contextlib import ExitStack

import numpy as np
import concourse.bass as bass
import concourse.mybir as mybir
from concourse.bass_utils import run_bass_kernel_spmd

F32 = mybir.dt.float32
BF16 = mybir.dt.bfloat16
AF = mybir.ActivationFunctionType
ALU = mybir.AluOpType

NCORES = 8
D = 1024
SEQ = 2048
BPC = 4
TOK = BPC * SEQ
T = 512
NT = TOK // T
TPS = SEQ // T
MEM = 256
DFF = 2816
NJ = DFF // 128
EPS = 1e-6

ORDER_IN = [14, 15, 12, 4, 8, 0, 13, 5, 9, 1, 6, 10, 2, 7, 11, 3]
N_IN = 16
N_G = 32
N_O = 8
N_UP = 44
N_DN = 22
NCHUNK = N_IN + N_G + N_O + N_UP + N_DN
NSLOT = 12

C_G1, C_G2, C_G3, C_G4 = 0, 1024, 2048, 3072
C_CW = 4096
C_FCW = C_CW + 12
C_BG = C_FCW + 132
C_PS = C_BG + 24
C_GM = C_PS + 2
C_BETA = C_GM + 8
C_RCW = C_BETA + 2
C_RC = C_RCW + 2
C_BD = C_RC + 32
CP = ((C_BD + 256 + 15) // 16) * 16


class Buf:
    def __init__(self, name):
        self.name = name
        self.last_w = None
        self.readers = []
        self.dma_sem = None
        self.dma_cnt = 0


class Op:
    __slots__ = ("eng", "fn", "deps", "needed", "sig", "is_dma", "dma_ev")

    def __init__(self, eng, fn):
        self.eng = eng
        self.fn = fn
        self.deps = []
        self.needed = False
        self.sig = None
        self.is_dma = False
        self.dma_ev = None


class Sched:
    def __init__(self, nc, es):
        self.nc = nc
        self.es = es
        self.ops = []
        self.engs = {"pe": nc.tensor, "act": nc.scalar, "dve": nc.vector, "pool": nc.gpsimd, "sp": nc.sync}
        self.sems = {k: es.enter_context(nc.semaphore("clk_" + k)) for k in self.engs}

    def _dep(self, op, prod):
        if prod is None or prod is op:
            return
        if (not prod.is_dma) and prod.eng == op.eng:
            return
        op.deps.append(prod)
        if not prod.is_dma:
            prod.needed = True

    def _track(self, op, reads, writes):
        for b in reads:
            self._dep(op, b.last_w)
        for b in writes:
            self._dep(op, b.last_w)
            for r in b.readers:
                self._dep(op, r)
        for b in reads:
            b.readers.append(op)
        for b in writes:
            b.last_w = op
            b.readers = []

    def op(self, eng, fn, reads=(), writes=()):
        o = Op(eng, fn)
        self._track(o, reads, writes)
        self.ops.append(o)
        return o

    def dma(self, queue, out_ap, in_ap, reads, writes, sem_buf):
        o = Op(queue, None)
        o.is_dma = True
        if sem_buf.dma_sem is None:
            sem_buf.dma_sem = self.es.enter_context(self.nc.semaphore("d_" + sem_buf.name))
        sem_buf.dma_cnt += 16
        o.dma_ev = (sem_buf.dma_sem, sem_buf.dma_cnt)
        eng = self.engs[queue]
        o.fn = lambda: eng.dma_start(out=out_ap, in_=in_ap)
        self._track(o, reads, writes)
        self.ops.append(o)
        return o

    def finalize(self, final_waits):
        cnt = {k: 0 for k in self.engs}
        for o in self.ops:
            if (not o.is_dma) and o.needed:
                cnt[o.eng] += 1
                o.sig = cnt[o.eng]
        known = {k: {} for k in self.engs}
        for o in self.ops:
            eng = self.engs[o.eng]
            need = {}
            for p in o.deps:
                if p.is_dma:
                    sem, val = p.dma_ev
                else:
                    sem, val = self.sems[p.eng], p.sig
                key = id(sem)
                if key not in need or need[key][1] < val:
                    need[key] = (sem, val)
            kn = known[o.eng]
            for key, (sem, val) in need.items():
                if kn.get(key, 0) < val:
                    eng.wait_ge(sem, val)
                    kn[key] = val
            ins = o.fn()
            if o.is_dma:
                ins.then_inc(o.dma_ev[0], 16)
            elif o.needed:
                ins.then_inc(self.sems[o.eng], 1)
        for queue, buf in final_waits:
            if buf.dma_sem is not None:
                self.engs[queue].wait_ge(buf.dma_sem, buf.dma_cnt)


class Ring:
    def __init__(self, items):
        self.items = items
        self.i = 0
        self.hist = []

    def next(self):
        it = self.items[self.i % len(self.items)]
        self.i += 1
        self.hist.append(it)
        return it


class PlanSched:
    def op(self, *a, **k):
        return None

    def dma(self, *a, **k):
        return None


def run_interleaved(main, side=None, after=()):
    side = iter(side) if side is not None else None
    for idx, _ in enumerate(main):
        if side is not None and idx in after:
            next(side, None)
    if side is not None:
        for _ in side:
            pass


def build_program():
    nc = bass.Bass("TRN2", target_bir_lowering=False)
    x_d = nc.dram_tensor("x", [TOK, D], F32, kind="ExternalInput").ap()
    mem_d = nc.dram_tensor("mem", [BPC * MEM, D], F32, kind="ExternalInput").ap()
    wf_d = nc.dram_tensor("wf", [NCHUNK * 128, 1024], F32, kind="ExternalInput").ap()
    wkv_d = nc.dram_tensor("wkv", [128, 8 * 512], F32, kind="ExternalInput").ap()
    cp_d = nc.dram_tensor("cpack", [128, CP], F32, kind="ExternalInput").ap()
    out_d = nc.dram_tensor("out", [TOK, D], F32, kind="ExternalOutput").ap()
    wb_d = nc.dram_tensor("wb", [NCHUNK * 128, 1024], BF16).ap()

    with ExitStack() as es:
        def sb(name, shape, dt):
            return es.enter_context(nc.sbuf_tensor(name, shape, dt))

        NX = 8
        xring_t = sb("xring", [128, NX, D], F32)
        hT_t = sb("hT", [128, 2, 8, T], BF16)
        h2T_t = sb("h2T", [128, 8, T], BF16)
        ff_t = sb("ffbuf", [128, NJ, T], BF16)
        ycat_t = sb("ycat", [128, 8, T], BF16)
        mrg_t = sb("merged", [128, 8, T], BF16)
        NGATE = 4
        gates_t = sb("gates", [128, NGATE, T], F32)
        NW32 = 7
        W32 = 528
        w32_t = sb("work32", [128, NW32, W32], F32)
        cv_t = sb("cv", [128, 4, 528], F32)
        ub_t = sb("ub", [128, 2, W32], F32)
        NHB = 2
        hb_t = sb("hb", [128, NHB, D], BF16)
        NTB = 2
        tb_t = sb("tbuf", [128, NTB, D], F32)
        NWB = 10
        wbf_t = sb("workbf", [128, NWB, T], BF16)
        cp_t = sb("cpk", [128, CP], F32)
        ws_t = sb("wslots", [128, NSLOT, 1024], BF16)
        wkv_t = sb("wkvbf", [128, 8, 512], BF16)
        kT_t = sb("kT", [128, 2, MEM], BF16)
        vpad_t = sb("vpad", [128, 2, 4, 128], BF16)
        ident_t = sb("ident", [128, 128], BF16)
        ones_t = sb("onesAB", [128, 2, 128], BF16)
        bd_t = sb("bdbf", [128, 2, 128], BF16)
        hbg_t = sb("hbgate", [128, 32], F32)
        st_t = sb("stats", [128, 64], F32)
        H_t = sb("ffnH", [128, 48, 2], F32)
        cy_t = sb("ffncarry", [128, 48, 3], F32)
        ps_t = es.enter_context(nc.psum_tensor("psall", [128, 8, 512], F32))

        pe, act, dve, pool = nc.tensor, nc.scalar, nc.vector, nc.gpsimd

        def bank(i):
            return ps_t[:, i, :]

        def bank_bf(i):
            return ps_t[:, i, :].bitcast(BF16)

        def bank2(i):
            return ps_t[:, i:i + 2, :].rearrange("p a b -> p (a b)")

        def col(c, n=1):
            return cp_t[:, c:c + n]

        eps_ap = st_t[:, 60:61]
        zero_ap = st_t[:, 62:63]
        negln2_ap = st_t[:, 63:64]

        def emit_all(S, plan, wseq):
            xr_b = [Buf(f"xr{i}") for i in range(NX)]
            hT_b = [Buf("hT0"), Buf("hT1")]
            h2T_b = Buf("h2T")
            ff_b = [Buf(f"ff{i}") for i in range(NJ)]
            ycat_b = [Buf(f"yc{i}") for i in range(8)]
            mrg_b = [Buf(f"mg{i}") for i in range(8)]
            gates_b = [Buf(f"gt{i}") for i in range(NGATE)]
            w32_b = [Buf(f"w32_{i}") for i in range(NW32)]
            cv_b = [Buf(f"cv{i}") for i in range(4)]
            ub_b = [Buf(f"ub{i}") for i in range(2)]
            hb_b = [Buf(f"hb{i}") for i in range(NHB)]
            tb_b = [Buf(f"tb{i}") for i in range(NTB)]
            wbf_b = [Buf(f"wbf{i}") for i in range(NWB)]
            cp_b = Buf("cpk")
            cpg1_b = Buf("cpg1")
            ws_b = [Buf(f"ws{i}") for i in range(NSLOT)]
            wkv_b = Buf("wkv")
            kT_b = Buf("kT")
            vpad_b = Buf("vpad")
            cst_b = Buf("consts")
            st_b = Buf("stats")
            H_b = Buf("ffnH")
            cy_b = Buf("ffncarry")
            bank_b = [Buf(f"bank{i}") for i in range(8)]
            NST = 19
            dummy_b = Buf("tblwarm")
            sts_b = [Buf(f"st{i}") for i in range(NST)]

            gates_r = Ring(list(range(NGATE)))
            w32_r = Ring(list(range(NW32)))
            hb_r = Ring(list(range(NHB)))
            tb_r = Ring(list(range(NTB)))
            wbf_r = Ring(list(range(NWB)))
            xr_r = Ring(list(range(NX)))
            bank_r = Ring(list(range(8)))
            stat_r = Ring(list(range(NST)))
            xs = {}

            wstate = {"issued": 0, "consumed": 0}

            stored = set()
            wbc_b = [Buf(f"wbc{i}") for i in range(NCHUNK)]

            def issue_weights(upto):
                while wstate["issued"] < min(upto, len(wseq)):
                    n = wstate["issued"]
                    c = wseq[n]
                    s = n % NSLOT
                    if c not in stored:
                        stored.add(c)
                        S.dma("pool", ws_t[:, s, :], wf_d[c * 128:(c + 1) * 128, :], [], [ws_b[s]], ws_b[s])
                        S.dma("sp", wb_d[c * 128:(c + 1) * 128, :], ws_t[:, s, :], [ws_b[s]], [wbc_b[c]], ws_b[s])
                    else:
                        S.dma("sp", ws_t[:, s, :], wb_d[c * 128:(c + 1) * 128, :], [wbc_b[c]], [ws_b[s]], ws_b[s])
                    wstate["issued"] += 1

            def next_w(cid):
                n = wstate["consumed"]
                wstate["consumed"] += 1
                if plan:
                    wseq.append(cid)
                    return n % NSLOT
                assert wseq[n] == cid
                issue_weights(n + NSLOT - 2)
                return n % NSLOT

            def wchunk(s, k):
                return ws_t[:, s, k * 128:(k + 1) * 128]

            def mm_chunk(s, rhs_list, rhs_bufs, bi, n=T):
                def fn():
                    last = None
                    nk = len(rhs_list)
                    for idx, (k, rhs) in enumerate(rhs_list):
                        last = pe.matmul(ps_t[:, bi, 0:n], lhsT=wchunk(s, k), rhs=rhs, start=(idx == 0), stop=(idx == nk - 1))
                    return last
                S.op("pe", fn, [ws_b[s]] + rhs_bufs, [bank_b[bi]])

            def rstd_of(src_ap, src_bufs, sq_scale, exp_bias, junk_ap, junk_buf):
                si = stat_r.next()
                c = 3 * si
                sbuf = sts_b[si]
                S.op("act", lambda: act.activation(out=junk_ap, in_=src_ap, func=AF.Square, scale=sq_scale,
                                                   accum_out=st_t[:, c:c + 1]),
                     src_bufs, [junk_buf, sbuf])
                S.op("act", lambda: act.activation(out=st_t[:, c + 1:c + 2], in_=st_t[:, c:c + 1], func=AF.Ln, bias=eps_ap, scale=1.0),
                     [sbuf, st_b], [sbuf])
                S.op("act", lambda: act.activation(out=st_t[:, c + 2:c + 3], in_=st_t[:, c + 1:c + 2], func=AF.Exp, scale=-0.5, bias=exp_bias),
                     [sbuf, st_b], [sbuf])
                return st_t[:, c + 2:c + 3], sbuf

            def norm_A(src_ap, src_bufs, gcol):
                hi = hb_r.next()
                r, rb = rstd_of(src_ap, src_bufs, 1.0 / 32.0, zero_ap, hb_t[:, hi, :], hb_b[hi])
                if gcol is None:
                    S.op("dve", lambda: dve.tensor_scalar(out=hb_t[:, hi, :], in0=src_ap, scalar1=r, scalar2=None, op0=ALU.mult),
                         src_bufs + [rb], [hb_b[hi]])
                else:
                    S.op("dve", lambda: dve.scalar_tensor_tensor(out=hb_t[:, hi, :], in0=src_ap, scalar=r, in1=cp_t[:, gcol:gcol + D],
                                                                 op0=ALU.mult, op1=ALU.mult),
                         src_bufs + [rb, cpg1_b if gcol == C_G1 else cp_b], [hb_b[hi]])
                return hi

            def norm_B(hi, dst3, dst_b, c0, c1, scale_col=None):
                bi = bank_r.next()

                def tr():
                    last = None
                    for k in range(8):
                        last = pe.transpose(bank_bf(bi)[:, k * 128:(k + 1) * 128], hb_t[:, hi, k * 128:(k + 1) * 128], ident_t[:])
                    return last
                S.op("pe", tr, [hb_b[hi], cst_b], [bank_b[bi]])
                if scale_col is None:
                    S.op("act", lambda: act.copy(out=dst3[:, :, c0:c1], in_=bank_bf(bi).rearrange("p (k t) -> p k t", k=8)),
                         [bank_b[bi]], [dst_b])
                else:
                    def ev():
                        last = None
                        for k in range(8):
                            last = act.activation(out=dst3[:, k, c0:c1], in_=bank_bf(bi)[:, k * 128:(k + 1) * 128], func=AF.Copy,
                                                  scale=col(scale_col + k))
                        return last
                    S.op("act", ev, [bank_b[bi], cp_b], [dst_b])

            def gen_norm(blocks, gcol, dst3, dst_b, scale_col=None):
                pend = None
                for (src_ap, src_bufs, c0, c1) in blocks:
                    hi = norm_A(src_ap, src_bufs, gcol)
                    yield
                    if pend is not None:
                        norm_B(pend[0], dst3, dst_b, pend[1], pend[2], scale_col)
                        yield
                    pend = (hi, c0, c1)
                norm_B(pend[0], dst3, dst_b, pend[1], pend[2], scale_col)
                yield

            kv_loads = {}

            def emit_KV_loads(ti, queue="pool"):
                seq = ti // TPS
                blocks = []
                for mb in range(2):
                    tb = tb_r.next()
                    r0 = seq * MEM + mb * 128
                    S.dma(queue, tb_t[:, tb, :], mem_d[r0:r0 + 128, :], [], [tb_b[tb]], tb_b[tb])
                    blocks.append((tb_t[:, tb, :], [tb_b[tb]], mb * 128, (mb + 1) * 128))
                kv_loads[ti] = blocks

            xs[0] = [xr_r.next() for _ in range(4)]

            def x0_load(b):
                xi = xs[0][b]
                S.dma("sp", xring_t[:, xi, :], x_d[b * 128:(b + 1) * 128, :], [], [xr_b[xi]], xr_b[xi])
            x0_load(0)
            S.dma("sp", cp_t[:, 0:1024], cp_d[:, 0:1024], [], [cpg1_b], cpg1_b)
            for b in range(1, 4):
                x0_load(b)
            emit_KV_loads(0, "sp")
            S.dma("sp", cp_t[:, 1024:CP], cp_d[:, 1024:CP], [], [cp_b], cp_b)

            def mk_consts():
                idf = w32_t[:, 0, 0:128]
                pool.memset(idf, 0.0)
                pool.affine_select(out=idf, in_=idf, pattern=[[-1, 128]],
                                   compare_op=ALU.not_equal, fill=1.0, base=0, channel_multiplier=1)
                pool.memset(ones_t[:], 0.0)
                pool.memset(ones_t[:, 0, 0:64], 1.0)
                pool.memset(ones_t[:, 1, 64:128], 1.0)
                pool.memset(vpad_t[:], 0.0)
                return pool.tensor_copy(out=ident_t[:], in_=idf)
            S.op("pool", mk_consts, [], [cst_b, vpad_b, w32_b[0]])
            w32_r.next()

            def mk_stat_consts():
                dve.memset(st_t[:], 0.0)
                dve.memset(st_t[:, 60:61], EPS)
                return dve.memset(st_t[:, 63:64], -math.log(2.0))
            S.op("dve", mk_stat_consts, [], [st_b] + sts_b)
            def table_warm():
                S.op("act", lambda: act.activation(out=st_t[:, 57:58], in_=st_t[:, 60:61], func=AF.Ln), [st_b], [dummy_b])
            table_warm()

            def emit_late_consts():
                S.op("dve", lambda: dve.tensor_copy(out=bd_t[:], in_=cp_t[:, C_BD:C_BD + 256].rearrange("p (a b) -> p a b", a=2)),
                     [cp_b], [cst_b])
                S.op("dve", lambda: dve.tensor_scalar(out=hbg_t[:, 0:24], in0=col(C_BG, 24), scalar1=0.5, scalar2=None, op0=ALU.mult),
                     [cp_b], [cst_b])

            def emit_xload(ti, queue="pool"):
                xs[ti] = []
                for b in range(4):
                    xi = xr_r.next()
                    xs[ti].append(xi)
                    r0 = ti * T + b * 128
                    S.dma(queue, xring_t[:, xi, :], x_d[r0:r0 + 128, :], [], [xr_b[xi]], xr_b[xi])

            def gen_N1(ti):
                hbuf = ti % 2
                blocks = [(xring_t[:, xs[ti][b], :], [xr_b[xs[ti][b]]], b * 128, (b + 1) * 128) for b in range(4)]
                yield from gen_norm(blocks, C_G1, hT_t[:, hbuf], hT_b[hbuf])

            def gen_KV(ti, hbuf=None):
                seq = ti // TPS
                hbuf = ti % 2 if hbuf is None else hbuf
                memT = hT_t[:, hbuf]
                memT_b = hT_b[hbuf]
                if ti not in kv_loads:
                    emit_KV_loads(ti)
                blocks = kv_loads[ti]
                yield from gen_norm(blocks, None, memT, memT_b, scale_col=C_GM)
                for c in range(2):
                    bi = bank_r.next()

                    def kfn(c=c, bi=bi):
                        last = None
                        for k in range(8):
                            last = pe.matmul(ps_t[:, bi, 0:MEM], lhsT=wkv_t[:, k, c * 128:(c + 1) * 128], rhs=memT[:, k, 0:MEM],
                                             start=(k == 0), stop=(k == 7))
                        return last
                    S.op("pe", kfn, [wkv_b, memT_b], [bank_b[bi]])
                    S.op("act", lambda c=c, bi=bi: act.copy(out=kT_t[:, c, :], in_=ps_t[:, bi, 0:MEM]), [bank_b[bi]], [kT_b])
                yield
                for mb in range(2):
                    bi = bank_r.next()

                    def vfn(mb=mb, bi=bi):
                        last = None
                        for k in range(8):
                            last = pe.matmul(ps_t[:, bi, 0:256], lhsT=memT[:, k, mb * 128:(mb + 1) * 128], rhs=wkv_t[:, k, 256:512],
                                             start=(k == 0), stop=(k == 7))
                        return last
                    S.op("pe", vfn, [wkv_b, memT_b], [bank_b[bi]])

                    def vcp(mb=mb, bi=bi):
                        src = ps_t[:, bi, 0:256].rearrange("p (j e d) -> p j e d", j=2, e=2)
                        dst = vpad_t[:, mb, :, :].rearrange("p (j e) (f d) -> p j e f d", j=2, f=2)
                        dve.tensor_copy(out=dst[:, :, 0, 0, :], in_=src[:, :, 0, :])
                        return dve.tensor_copy(out=dst[:, :, 1, 1, :], in_=src[:, :, 1, :])
                    S.op("dve", vcp, [bank_b[bi]], [vpad_b])
                yield

            def gen_M1(ti):
                first = (ti % TPS == 0)
                hbuf = ti % 2
                hT_rhs = [(k, hT_t[:, hbuf, k, :]) for k in range(8)]
                hTb = hT_b[hbuf]
                for q in range(4):
                    if first:
                        S.op("dve", lambda q=q: dve.memset(cv_t[:, q, 0:2], 0.0), [], [cv_b[q]])
                    else:
                        S.op("dve", lambda q=q: dve.tensor_copy(out=cv_t[:, q, 0:2], in_=cv_t[:, q, 512:514]), [cv_b[q]], [cv_b[q]])
                for j in range(2):
                    if first:
                        S.op("dve", lambda j=j: dve.memset(ub_t[:, j, 0:16], 0.0), [], [ub_b[j]])
                    else:
                        S.op("dve", lambda j=j: dve.tensor_copy(out=ub_t[:, j, 0:16], in_=ub_t[:, j, 512:528]), [ub_b[j]], [ub_b[j]])
                csb = {}
                tconv = {}
                att = {}
                att_pts = {}
                pool_pending = {}
                for pos, cid in enumerate(ORDER_IN):
                    s = next_w(pos)
                    bi = bank_r.next()
                    mm_chunk(s, hT_rhs, [hTb], bi)
                    if 4 <= cid < 8:
                        q = cid - 4
                        wi = w32_r.next()
                        csb[q] = wi
                        S.op("act", lambda bi=bi, wi=wi: act.copy(out=w32_t[:, wi, 0:T], in_=bank(bi)), [bank_b[bi]], [w32_b[wi]])
                    elif 8 <= cid < 12:
                        q = cid - 8
                        wi = csb[q]
                        S.op("dve", lambda bi=bi, wi=wi, q=q: dve.tensor_tensor(out=cv_t[:, q, 2:514], in0=bank(bi), in1=w32_t[:, wi, 0:T], op=ALU.mult),
                             [bank_b[bi], w32_b[wi]], [cv_b[q]])
                        wt = w32_r.next()
                        tconv[q] = wt
                        S.op("act", lambda q=q, wt=wt: act.activation(out=w32_t[:, wt, 0:T], in_=cv_t[:, q, 0:T], func=AF.Copy, scale=col(C_CW + 3 * q)),
                             [cv_b[q], cp_b], [w32_b[wt]])
                        S.op("dve", lambda q=q, wt=wt: dve.scalar_tensor_tensor(out=w32_t[:, wt, 0:T], in0=cv_t[:, q, 1:T + 1], scalar=col(C_CW + 3 * q + 1),
                                                                                in1=w32_t[:, wt, 0:T], op0=ALU.mult, op1=ALU.add),
                             [cv_b[q], cp_b, w32_b[wt]], [w32_b[wt]])
                        S.op("dve", lambda q=q, wt=wt: dve.scalar_tensor_tensor(out=w32_t[:, wt, 0:T], in0=cv_t[:, q, 2:T + 2], scalar=col(C_CW + 3 * q + 2),
                                                                                in1=w32_t[:, wt, 0:T], op0=ALU.mult, op1=ALU.add),
                             [cv_b[q], cp_b, w32_b[wt]], [w32_b[wt]])
                    elif cid < 4:
                        q = cid
                        wt = tconv[q]
                        S.op("dve", lambda bi=bi, wt=wt, q=q: dve.tensor_tensor(out=ycat_t[:, q, :], in0=bank(bi), in1=w32_t[:, wt, 0:T], op=ALU.mult),
                             [bank_b[bi], w32_b[wt]], [ycat_b[q]])
                    elif cid < 14:
                        j = cid - 12
                        S.op("act", lambda bi=bi, j=j: act.copy(out=ub_t[:, j, 16:528], in_=bank(bi)), [bank_b[bi]], [ub_b[j]])
                        U = ub_t[:, j, :]
                        a = w32_r.next()
                        S.op("dve", lambda a=a, U=U: dve.tensor_tensor(out=w32_t[:, a, 1:528], in0=U[:, 1:528], in1=U[:, 0:527], op=ALU.add),
                             [ub_b[j]], [w32_b[a]])
                        lo = 1
                        cur = a
                        if j == 1:
                            for sh in (2, 4):
                                nb = w32_r.next()
                                S.op("dve", lambda cur=cur, nb=nb, sh=sh, lo=lo: dve.tensor_tensor(
                                    out=w32_t[:, nb, lo + sh:528], in0=w32_t[:, cur, lo + sh:528], in1=w32_t[:, cur, lo:528 - sh], op=ALU.add),
                                    [w32_b[cur]], [w32_b[nb]])
                                cur = nb
                                lo += sh
                        sh = 2 if j == 0 else 8
                        wg = w32_r.next()
                        S.op("dve", lambda cur=cur, wg=wg, sh=sh, lo=lo, j=j: dve.scalar_tensor_tensor(
                            out=w32_t[:, wg, lo + sh:528], in0=w32_t[:, cur, lo:528 - sh], scalar=col(C_BETA + j),
                            in1=w32_t[:, cur, lo + sh:528], op0=ALU.mult, op1=ALU.add),
                            [w32_b[cur], cp_b], [w32_b[wg]])
                        pb = wbf_r.next()
                        if first:
                            S.op("dve", lambda wg=wg, j=j: dve.tensor_tensor(out=w32_t[:, wg, 0:16], in0=w32_t[:, wg, 16:32],
                                                                              in1=cp_t[:, C_RC + 16 * j:C_RC + 16 * j + 16], op=ALU.mult),
                                 [w32_b[wg], cp_b], [w32_b[wg]])
                        S.op("dve", lambda wg=wg, pb=pb, j=j, U=U: dve.scalar_tensor_tensor(
                            out=wbf_t[:, pb, :], in0=w32_t[:, wg, 16:528], scalar=col(C_RCW + j), in1=U[:, 16:528],
                            op0=ALU.mult, op1=ALU.subtract),
                            [w32_b[wg], cp_b, ub_b[j]], [wbf_b[pb]])
                        if first:
                            S.op("dve", lambda wg=wg, pb=pb, U=U: dve.tensor_tensor(out=wbf_t[:, pb, 0:16], in0=w32_t[:, wg, 0:16], in1=U[:, 16:32],
                                                                                   op=ALU.subtract),
                                 [w32_b[wg], ub_b[j], wbf_b[pb]], [wbf_b[pb]])
                        def pool_fin(pb=pb, j=j):
                            b2 = bank_r.next()
                            S.op("pe", lambda: pe.matmul(bank(b2), lhsT=bd_t[:, j, :], rhs=wbf_t[:, pb, :], start=True, stop=True),
                                 [cst_b, wbf_b[pb]], [bank_b[b2]])
                            S.op("act", lambda: act.activation(out=ycat_t[:, 4 + j, :], in_=bank(b2), func=AF.Copy, scale=col(C_PS + j)),
                                 [bank_b[b2], cp_b], [ycat_b[4 + j]])
                        pool_pending[j] = pool_fin
                    else:
                        j = cid - 14
                        qb = wbf_r.next()
                        S.op("act", lambda bi=bi, qb=qb: act.copy(out=wbf_t[:, qb, :], in_=bank(bi)), [bank_b[bi]], [wbf_b[qb]])

                        def stage1(j=j, qb=qb):
                            pts = []
                            for e in range(2):
                                for mc in range(2):
                                    b3 = bank_r.next()
                                    S.op("pe", lambda b3=b3, e=e, mc=mc: pe.matmul(
                                        bank(b3), lhsT=kT_t[e * 64:(e + 1) * 64, j, mc * 128:(mc + 1) * 128],
                                        rhs=wbf_t[e * 64:(e + 1) * 64, qb, :], start=True, stop=True),
                                        [kT_b, wbf_b[qb]], [bank_b[b3]])
                                    pt = wbf_r.next()
                                    S.op("act", lambda b3=b3, pt=pt: act.activation(out=wbf_t[:, pt, :], in_=bank(b3), func=AF.Exp, scale=0.125),
                                         [bank_b[b3]], [wbf_b[pt]])
                                    pts.append((e, mc, pt))
                            return pts

                        def stage2(pts, j=j):
                            bpv = bank_r.next()
                            bdn = bank_r.next()

                            def pv():
                                last = None
                                for idx, (e, mc, pt) in enumerate(pts):
                                    last = pe.matmul(bank(bpv), lhsT=vpad_t[:, mc, 2 * j + e, :], rhs=wbf_t[:, pt, :], start=(idx == 0), stop=(idx == 3))
                                return last

                            def dn():
                                last = None
                                for idx, (e, mc, pt) in enumerate(pts):
                                    last = pe.matmul(bank(bdn), lhsT=ones_t[:, e, :], rhs=wbf_t[:, pt, :], start=(idx == 0), stop=(idx == 3))
                                return last
                            ptb = [wbf_b[p[2]] for p in pts]
                            S.op("pe", pv, [vpad_b] + ptb, [bank_b[bpv]])
                            S.op("pe", dn, [cst_b] + ptb, [bank_b[bdn]])
                            rd = w32_r.next()
                            S.op("dve", lambda: dve.reciprocal(out=w32_t[:, rd, 0:T], in_=bank(bdn)), [bank_b[bdn]], [w32_b[rd]])
                            S.op("dve", lambda: dve.tensor_tensor(out=ycat_t[:, 6 + j, :], in0=bank(bpv), in1=w32_t[:, rd, 0:T], op=ALU.mult),
                                 [bank_b[bpv], w32_b[rd]], [ycat_b[6 + j]])
                        att[j] = (stage1, stage2)
                    for (jj, st, at_pos) in ((0, 1, 3), (0, 2, 5), (1, 1, 6), (1, 2, 9)):
                        if pos == at_pos:
                            if st == 1:
                                att_pts[jj] = att[jj][0]()
                            else:
                                att[jj][1](att_pts[jj])
                    for (jj, at_pos) in ((0, 7), (1, 10)):
                        if pos == at_pos:
                            pool_pending[jj]()
                    yield

            def gen_G(ti):
                hbuf = ti % 2
                hT_rhs = [(k, hT_t[:, hbuf, k, :]) for k in range(8)]
                hTb = hT_b[hbuf]
                for j in range(8):
                    gts = []
                    for i in range(3):
                        s = next_w(N_IN + 4 * j + i)
                        bi = bank_r.next()
                        mm_chunk(s, hT_rhs, [hTb], bi)
                        gi = gates_r.next()
                        gts.append(gi)
                        S.op("act", lambda bi=bi, gi=gi, i=i, j=j: act.activation(out=gates_t[:, gi, :], in_=bank(bi), func=AF.Tanh,
                                                                                  bias=hbg_t[:, i * 8 + j:i * 8 + j + 1], scale=0.5),
                             [bank_b[bi], cst_b], [gates_b[gi]])
                    s = next_w(N_IN + 4 * j + 3)
                    pbanks = []
                    for i, ks in enumerate(([0, 1, 2, 3], [4, 5], [6, 7])):
                        bi = bank_r.next()
                        pbanks.append(bi)
                        mm_chunk(s, [(k, ycat_t[:, k, :]) for k in ks], [ycat_b[k] for k in ks], bi)
                    m0 = w32_r.next()
                    m1 = w32_r.next()
                    S.op("dve", lambda gi=gts[0], bi=pbanks[0], m0=m0: dve.scalar_tensor_tensor(
                        out=w32_t[:, m0, 0:T], in0=gates_t[:, gi, :], scalar=1.0, in1=bank(bi), op0=ALU.add, op1=ALU.mult),
                        [gates_b[gts[0]], bank_b[pbanks[0]]], [w32_b[m0]])
                    S.op("dve", lambda gi=gts[1], bi=pbanks[1], m1=m1: dve.scalar_tensor_tensor(
                        out=w32_t[:, m1, 0:T], in0=gates_t[:, gi, :], scalar=1.0, in1=bank(bi), op0=ALU.add, op1=ALU.mult),
                        [gates_b[gts[1]], bank_b[pbanks[1]]], [w32_b[m1]])
                    S.op("dve", lambda m0=m0, m1=m1: dve.tensor_tensor(out=w32_t[:, m0, 0:T], in0=w32_t[:, m0, 0:T], in1=w32_t[:, m1, 0:T], op=ALU.add),
                         [w32_b[m0], w32_b[m1]], [w32_b[m0]])
                    S.op("dve", lambda gi=gts[2], bi=pbanks[2], m1=m1: dve.scalar_tensor_tensor(
                        out=w32_t[:, m1, 0:T], in0=gates_t[:, gi, :], scalar=1.0, in1=bank(bi), op0=ALU.add, op1=ALU.mult),
                        [gates_b[gts[2]], bank_b[pbanks[2]]], [w32_b[m1]])
                    S.op("dve", lambda m0=m0, m1=m1, j=j: dve.tensor_tensor(out=mrg_t[:, j, :], in0=w32_t[:, m0, 0:T], in1=w32_t[:, m1, 0:T], op=ALU.add),
                         [w32_b[m0], w32_b[m1]], [mrg_b[j]])
                    yield
                table_warm()

            def acc_phase_emit(cid0, nk, lhs_fn, lhs_bufs_fn, KK=3):
                slots = {}

                def edge(ks, order=(0, 1, 2, 3)):
                    for k in ks:
                        slots[k] = next_w(cid0 + k)
                    for b in order:
                        def fn(b=b):
                            last = None
                            for k in ks:
                                for nh in range(2):
                                    last = pe.matmul(ps_t[:, 2 * b + nh, :], lhsT=lhs_fn(k, b), rhs=ws_t[:, slots[k], nh * 512:(nh + 1) * 512],
                                                     start=(k == 0), stop=(k == nk - 1))
                            return last
                        rb = [ws_b[slots[k]] for k in ks]
                        for k in ks:
                            rb = rb + lhs_bufs_fn(k)
                        S.op("pe", fn, rb, [bank_b[2 * b], bank_b[2 * b + 1]])
                recent = bank_r.hist[-8:]
                age = lambda b: max([len(recent) - 1 - recent[::-1].index(x) if x in recent else -1 for x in (2 * b, 2 * b + 1)])
                edge(list(range(KK)), order=sorted(range(4), key=age))
                for k in range(KK, nk - KK):
                    s = next_w(cid0 + k)

                    def fn(k=k, s=s):
                        last = None
                        for b in range(4):
                            for nh in range(2):
                                last = pe.matmul(ps_t[:, 2 * b + nh, :], lhsT=lhs_fn(k, b), rhs=ws_t[:, s, nh * 512:(nh + 1) * 512],
                                                 start=False, stop=False)
                        return last
                    S.op("pe", fn, [ws_b[s]] + lhs_bufs_fn(k), list(bank_b))
                edge(list(range(nk - KK, nk)))
                bank_r.i = 0

            def post_norm(ti, b, sq_scale, exp_bias, gcol, add_eng="dve"):
                xi = xs[ti][b]
                ah = pool if add_eng == "pool" else dve
                o_ap = bank2(2 * b)
                tb = tb_r.next()
                r, rb = rstd_of(o_ap, [bank_b[2 * b], bank_b[2 * b + 1]], sq_scale, exp_bias, tb_t[:, tb, :], tb_b[tb])
                S.op("dve", lambda: dve.scalar_tensor_tensor(out=tb_t[:, tb, :], in0=o_ap, scalar=r,
                                                             in1=cp_t[:, gcol:gcol + D], op0=ALU.mult, op1=ALU.mult),
                     [bank_b[2 * b], bank_b[2 * b + 1], rb, cp_b], [tb_b[tb]])
                S.op(add_eng, lambda: ah.tensor_tensor(out=xring_t[:, xi, :], in0=xring_t[:, xi, :], in1=tb_t[:, tb, :], op=ALU.add),
                     [xr_b[xi], tb_b[tb]], [xr_b[xi]])

            def emit_O(ti):
                acc_phase_emit(N_IN + N_G, 8, lambda k, b: mrg_t[:, k, b * 128:(b + 1) * 128], lambda k: [mrg_b[k]])

            def emit_N2a(ti):
                for b in range(4):
                    post_norm(ti, b, 1.0 / 64.0, negln2_ap, C_G2, add_eng=("pool" if (ti >= 1 and b >= 2) else "dve"))

            def gen_N2b(ti):
                blocks = [(xring_t[:, xs[ti][b], :], [xr_b[xs[ti][b]]], b * 128, (b + 1) * 128) for b in range(4)]
                yield from gen_norm(blocks, C_G3, h2T_t, h2T_b)

            def gen_F(ti):
                first = (ti % TPS == 0)
                h2_rhs = [(k, h2T_t[:, k, :]) for k in range(8)]
                bank_r.i = 4
                for j in range(NJ):
                    obuf = {}
                    for widx, (which, ch) in enumerate((("g", j), ("v", NJ + j))):
                        s = next_w(N_IN + N_G + N_O + 2 * j + widx)
                        bi = bank_r.next()
                        mm_chunk(s, h2_rhs, [h2T_b], bi)
                        ob = w32_r.next()
                        obuf[which] = ob
                        cw = C_FCW + 3 * ch
                        def ev(bi=bi, ob=ob, cw=cw, ch=ch, first=first):
                            if first:
                                act.activation(out=w32_t[:, ob, 0:T], in_=bank(bi), func=AF.Copy, scale=col(cw + 2))
                            else:
                                act.activation(out=w32_t[:, ob, 0:1], in_=ps_t[:, bi, 0:1], func=AF.Identity, bias=cy_t[:, ch, 0:1], scale=col(cw + 2))
                                act.activation(out=w32_t[:, ob, 1:2], in_=ps_t[:, bi, 1:2], func=AF.Identity, bias=cy_t[:, ch, 1:2], scale=col(cw + 2))
                                act.activation(out=w32_t[:, ob, 2:T], in_=ps_t[:, bi, 2:T], func=AF.Copy, scale=col(cw + 2))
                            return act.copy(out=H_t[:, ch, :], in_=ps_t[:, bi, T - 2:T])
                        S.op("act", ev, [bank_b[bi], cp_b] + ([] if first else [cy_b]), [w32_b[ob], H_b])
                        S.op("dve", lambda bi=bi, ob=ob, cw=cw: dve.scalar_tensor_tensor(out=w32_t[:, ob, 1:T], in0=ps_t[:, bi, 0:T - 1], scalar=col(cw + 1),
                                                                                         in1=w32_t[:, ob, 1:T], op0=ALU.mult, op1=ALU.add),
                             [bank_b[bi], cp_b, w32_b[ob]], [w32_b[ob]])
                        S.op("dve", lambda bi=bi, ob=ob, cw=cw: dve.scalar_tensor_tensor(out=w32_t[:, ob, 2:T], in0=ps_t[:, bi, 0:T - 2], scalar=col(cw),
                                                                                         in1=w32_t[:, ob, 2:T], op0=ALU.mult, op1=ALU.add),
                             [bank_b[bi], cp_b, w32_b[ob]], [w32_b[ob]])
                    og, ov = obuf["g"], obuf["v"]
                    S.op("act", lambda og=og: act.activation(out=w32_t[:, og, 0:T], in_=w32_t[:, og, 0:T], func=AF.Gelu_apprx_tanh),
                         [w32_b[og]], [w32_b[og]])
                    S.op("dve", lambda og=og, ov=ov, j=j: dve.tensor_tensor(out=ff_t[:, j, :], in0=w32_t[:, og, 0:T], in1=w32_t[:, ov, 0:T], op=ALU.mult),
                         [w32_b[og], w32_b[ov]], [ff_b[j]])
                    yield
                if ti % TPS != TPS - 1:
                    def carry1():
                        fw = cp_t[:, C_FCW:C_FCW + 132].rearrange("p (c k) -> p c k", k=3)
                        dve.tensor_tensor(out=cy_t[:, 0:N_UP, 0], in0=fw[:, :, 1], in1=H_t[:, 0:N_UP, 1], op=ALU.mult)
                        dve.tensor_tensor(out=cy_t[:, 0:N_UP, 2], in0=fw[:, :, 0], in1=H_t[:, 0:N_UP, 0], op=ALU.mult)
                        return dve.tensor_tensor(out=cy_t[:, 0:N_UP, 1], in0=fw[:, :, 0], in1=H_t[:, 0:N_UP, 1], op=ALU.mult)
                    S.op("dve", carry1, [H_b, cp_b, cy_b], [cy_b])
                table_warm()

            def emit_D(ti):
                acc_phase_emit(N_IN + N_G + N_O + N_UP, NJ, lambda k, b: ff_t[:, k, b * 128:(b + 1) * 128], lambda k: [ff_b[k]])

            def emit_Dpost(ti):
                need_carry = (ti % TPS != TPS - 1)
                for b in range(4):
                    post_norm(ti, b, 1.0 / 32.0, zero_ap, C_G4)
                    xi = xs[ti][b]
                    r0 = ti * T + b * 128
                    S.dma("pool", out_d[r0:r0 + 128, :], xring_t[:, xi, :], [xr_b[xi]], [], xr_b[xi])
                    if b == 0 and need_carry:
                        S.op("dve", lambda: dve.tensor_tensor(out=cy_t[:, 0:N_UP, 0], in0=cy_t[:, 0:N_UP, 0], in1=cy_t[:, 0:N_UP, 2], op=ALU.add),
                             [cy_b], [cy_b])

            issue_weights(2)
            S.dma("pool", wkv_t[:].rearrange("p a b -> p (a b)"), wkv_d, [], [wkv_b], wkv_b)
            run_interleaved(gen_N1(0))
            run_interleaved(gen_KV(0, hbuf=1))
            issue_weights(NSLOT - 2)
            emit_late_consts()
            if NT > 1:
                emit_xload(1, "sp")
            run_interleaved(gen_M1(0))
            if NT > 1:
                run_interleaved(gen_G(0), gen_N1(1), after={1, 2, 3, 4, 5, 6, 7})
            else:
                run_interleaved(gen_G(0))
            for i in range(NT):
                emit_O(i)
                emit_N2a(i)
                if i + 2 < NT and (i + 2) % TPS == 0:
                    emit_KV_loads(i + 2)
                if i + 1 < NT:
                    run_interleaved(gen_M1(i + 1), gen_N2b(i), after={0, 1, 4, 5, 7, 9, 11, 13})
                else:
                    run_interleaved(gen_N2b(i))
                if i + 2 < NT and (i + 2) % TPS == 0:
                    run_interleaved(gen_KV(i + 2))
                run_interleaved(gen_F(i))
                emit_D(i)
                emit_Dpost(i)
                if i + 2 < NT:
                    emit_xload(i + 2)
                if i + 1 < NT:
                    run_interleaved(gen_G(i + 1), gen_N1(i + 2) if i + 2 < NT else None, after={1, 2, 3, 4, 5, 6, 7})
            return xr_b

        wseq = []
        emit_all(PlanSched(), True, wseq)
        S = Sched(nc, es)
        xr_b = emit_all(S, False, wseq)
        S.finalize([("pool", b) for b in xr_b])
    return nc


_NC_CACHE = {}


def _type_a(W):
    K, C = W.shape
    assert K == 1024
    return W.reshape(8, 128, C // 128, 128).transpose(2, 1, 0, 3).reshape(C // 128, 128, 1024)


def kernel(x, mem, g_mix_pre, w_in, conv_w, pool_w, pool_scale, g_mem, w_kv,
           w_br_conv, w_br_pool, w_br_attn, w_gate, b_gate, w_o, g_mix_post,
           g_ffn_pre, w_up, ffn_conv_w, w_down, g_ffn_post):
    f = lambda a: np.asarray(a, dtype=np.float32)
    x, mem = f(x), f(mem)
    w_in, w_gate, w_o, w_up, w_down, w_kv = f(w_in)[0], f(w_gate)[0], f(w_o)[0], f(w_up)[0], f(w_down)[0], f(w_kv)[0]
    wbr = np.concatenate([f(w_br_conv)[0], f(w_br_pool)[0], f(w_br_attn)[0]], axis=0)

    a_in = _type_a(w_in)
    a_gate = _type_a(w_gate)
    a_br = _type_a(wbr)
    a_up = _type_a(w_up)
    chunks = [a_in[c] for c in ORDER_IN]
    for j in range(8):
        for i in range(3):
            chunks.append(a_gate[i * 8 + j])
        chunks.append(a_br[j])
    for k in range(8):
        chunks.append(w_o[k * 128:(k + 1) * 128, :])
    for j in range(NJ):
        chunks.append(a_up[j])
        chunks.append(a_up[NJ + j])
    for k in range(NJ):
        chunks.append(w_down[k * 128:(k + 1) * 128, :])
    wf = np.ascontiguousarray(np.stack(chunks, axis=0).reshape(NCHUNK * 128, 1024))
    wkv = np.ascontiguousarray(w_kv.reshape(8, 128, 512).transpose(1, 0, 2).reshape(128, 8 * 512))

    cp = np.zeros((128, CP), np.float32)
    cp[:, C_G1:C_G1 + D] = f(g_mix_pre)[0][None, :]
    cp[:, C_G2:C_G2 + D] = f(g_mix_post)[0][None, :]
    cp[:, C_G3:C_G3 + D] = f(g_ffn_pre)[0][None, :]
    cp[:, C_G4:C_G4 + D] = f(g_ffn_post)[0][None, :]
    cw = f(conv_w)[0]
    cp[:, C_CW:C_CW + 12] = cw.reshape(3, 4, 128).transpose(2, 1, 0).reshape(128, 12)
    fcw = f(ffn_conv_w)[0]
    cp[:, C_FCW:C_FCW + 132] = fcw.reshape(3, 44, 128).transpose(2, 1, 0).reshape(128, 132)
    cp[:, C_BG:C_BG + 24] = f(b_gate)[0].reshape(24, 128).T
    cp[:, C_PS:C_PS + 2] = f(pool_scale)[0].reshape(2, 128).T
    cp[:, C_GM:C_GM + 8] = f(g_mem)[0].reshape(8, 128).T
    wins = np.array([[2, 4], [8, 16]], np.float32)
    for j in range(2):
        cp[0:64, C_BETA + j] = 0.0
        cp[64:128, C_BETA + j] = 1.0
        cp[0:64, C_RCW + j] = 1.0 / wins[j, 0]
        cp[64:128, C_RCW + j] = 1.0 / wins[j, 1]
        t = np.arange(16, dtype=np.float32) + 1.0
        cp[0:64, C_RC + 16 * j:C_RC + 16 * j + 16] = 1.0 / np.minimum(wins[j, 0], t)
        cp[64:128, C_RC + 16 * j:C_RC + 16 * j + 16] = 1.0 / np.minimum(wins[j, 1], t)
    pw = f(pool_w)[0]
    bd = np.zeros((128, 2, 128), np.float32)
    for g in range(4):
        j, e = g // 2, g % 2
        bd[e * 64:(e + 1) * 64, j, e * 64:(e + 1) * 64] = pw[g]
    cp[:, C_BD:C_BD + 256] = bd.reshape(128, 256)

    if "nc" not in _NC_CACHE:
        _NC_CACHE["nc"] = build_program()
    nc = _NC_CACHE["nc"]
    in_maps = []
    for c in range(NCORES):
        in_maps.append({
            "x": np.ascontiguousarray(x[c * BPC:(c + 1) * BPC].reshape(TOK, D)),
            "mem": np.ascontiguousarray(mem[c * BPC:(c + 1) * BPC].reshape(BPC * MEM, D)),
            "wf": wf, "wkv": wkv, "cpack": cp,
        })
    res = run_bass_kernel_spmd(nc, in_maps, core_ids=list(range(NCORES)))
    out = np.stack([np.asarray(r["out"]).reshape(BPC, SEQ, D) for r in res.results], axis=0)
    return out.reshape(NCORES * BPC, SEQ, D).astype(np.float32)
```

```python
import math
from contextlib import ExitStack

import numpy as np
import concourse.bass as bass
import concourse.mybir as mybir
from concourse.bass_utils import run_bass_kernel_spmd

F32 = mybir.dt.float32
BF16 = mybir.dt.bfloat16
AF = mybir.ActivationFunctionType
ALU = mybir.AluOpType

NCORES = 8
D = 1024
SEQ = 2048
BPC = 4
TOK = BPC * SEQ
T = 512
NT = TOK // T
TPS = SEQ // T
MEM = 256
DFF = 2816
NJ = DFF // 128
EPS = 1e-6

ORDER_IN = [14, 15, 12, 4, 8, 0, 13, 5, 9, 1, 6, 10, 2, 7, 11, 3]
N_IN = 16
N_G = 32
N_O = 8
N_UP = 44
N_DN = 22
NCHUNK = N_IN + N_G + N_O + N_UP + N_DN
NSLOT = 12

C_G1, C_G2, C_G3, C_G4 = 0, 1024, 2048, 3072
C_CW = 4096
C_FCW = C_CW + 12
C_BG = C_FCW + 132
C_PS = C_BG + 24
C_GM = C_PS + 2
C_BETA = C_GM + 8
C_RCW = C_BETA + 2
C_RC = C_RCW + 2
C_BD = C_RC + 32
CP = ((C_BD + 256 + 15) // 16) * 16


class Buf:
    def __init__(self, name):
        self.name = name
        self.last_w = None
        self.readers = []
        self.dma_sem = None
        self.dma_cnt = 0


class Op:
    __slots__ = ("eng", "fn", "deps", "needed", "sig", "is_dma", "dma_ev")

    def __init__(self, eng, fn):
        self.eng = eng
        self.fn = fn
        self.deps = []
        self.needed = False
        self.sig = None
        self.is_dma = False
        self.dma_ev = None


class Sched:
    def __init__(self, nc, es):
        self.nc = nc
        self.es = es
        self.ops = []
        self.engs = {"pe": nc.tensor, "act": nc.scalar, "dve": nc.vector, "pool": nc.gpsimd, "sp": nc.sync}
        self.sems = {k: es.enter_context(nc.semaphore("clk_" + k)) for k in self.engs}

    def _dep(self, op, prod):
        if prod is None or prod is op:
            return
        if (not prod.is_dma) and prod.eng == op.eng:
            return
        op.deps.append(prod)
        if not prod.is_dma:
            prod.needed = True

    def _track(self, op, reads, writes):
        for b in reads:
            self._dep(op, b.last_w)
        for b in writes:
            self._dep(op, b.last_w)
            for r in b.readers:
                self._dep(op, r)
        for b in reads:
            b.readers.append(op)
        for b in writes:
            b.last_w = op
            b.readers = []

    def op(self, eng, fn, reads=(), writes=()):
        o = Op(eng, fn)
        self._track(o, reads, writes)
        self.ops.append(o)
        return o

    def dma(self, queue, out_ap, in_ap, reads, writes, sem_buf):
        o = Op(queue, None)
        o.is_dma = True
        if sem_buf.dma_sem is None:
            sem_buf.dma_sem = self.es.enter_context(self.nc.semaphore("d_" + sem_buf.name))
        sem_buf.dma_cnt += 16
        o.dma_ev = (sem_buf.dma_sem, sem_buf.dma_cnt)
        eng = self.engs[queue]
        o.fn = lambda: eng.dma_start(out=out_ap, in_=in_ap)
        self._track(o, reads, writes)
        self.ops.append(o)
        return o

    def finalize(self, final_waits):
        cnt = {k: 0 for k in self.engs}
        for o in self.ops:
            if (not o.is_dma) and o.needed:
                cnt[o.eng] += 1
                o.sig = cnt[o.eng]
        known = {k: {} for k in self.engs}
        for o in self.ops:
            eng = self.engs[o.eng]
            need = {}
            for p in o.deps:
                if p.is_dma:
                    sem, val = p.dma_ev
                else:
                    sem, val = self.sems[p.eng], p.sig
                key = id(sem)
                if key not in need or need[key][1] < val:
                    need[key] = (sem, val)
            kn = known[o.eng]
            for key, (sem, val) in need.items():
                if kn.get(key, 0) < val:
                    eng.wait_ge(sem, val)
                    kn[key] = val
            ins = o.fn()
            if o.is_dma:
                ins.then_inc(o.dma_ev[0], 16)
            elif o.needed:
                ins.then_inc(self.sems[o.eng], 1)
        for queue, buf in final_waits:
            if buf.dma_sem is not None:
                self.engs[queue].wait_ge(buf.dma_sem, buf.dma_cnt)


class Ring:
    def __init__(self, items):
        self.items = items
        self.i = 0
        self.hist = []

    def next(self):
        it = self.items[self.i % len(self.items)]
        self.i += 1
        self.hist.append(it)
        return it


class PlanSched:
    def op(self, *a, **k):
        return None

    def dma(self, *a, **k):
        return None


def run_interleaved(main, side=None, after=()):
    side = iter(side) if side is not None else None
    for idx, _ in enumerate(main):
        if side is not None and idx in after:
            next(side, None)
    if side is not None:
        for _ in side:
            pass


def build_program():
    nc = bass.Bass("TRN2", target_bir_lowering=False)
    x_d = nc.dram_tensor("x", [TOK, D], F32, kind="ExternalInput").ap()
    mem_d = nc.dram_tensor("mem", [BPC * MEM, D], F32, kind="ExternalInput").ap()
    wf_d = nc.dram_tensor("wf", [NCHUNK * 128, 1024], F32, kind="ExternalInput").ap()
    wkv_d = nc.dram_tensor("wkv", [128, 8 * 512], F32, kind="ExternalInput").ap()
    cp_d = nc.dram_tensor("cpack", [128, CP], F32, kind="ExternalInput").ap()
    out_d = nc.dram_tensor("out", [TOK, D], F32, kind="ExternalOutput").ap()
    wb_d = nc.dram_tensor("wb", [NCHUNK * 128, 1024], BF16).ap()

    with ExitStack() as es:
        def sb(name, shape, dt):
            return es.enter_context(nc.sbuf_tensor(name, shape, dt))

        NX = 8
        xring_t = sb("xring", [128, NX, D], F32)
        hT_t = sb("hT", [128, 2, 8, T], BF16)
        h2T_t = sb("h2T", [128, 8, T], BF16)
        ff_t = sb("ffbuf", [128, NJ, T], BF16)
        ycat_t = sb("ycat", [128, 8, T], BF16)
        mrg_t = sb("merged", [128, 8, T], BF16)
        NGATE = 4
        gates_t = sb("gates", [128, NGATE, T], F32)
        NW32 = 7
        W32 = 528
        w32_t = sb("work32", [128, NW32, W32], F32)
        cv_t = sb("cv", [128, 4, 528], F32)
        ub_t = sb("ub", [128, 2, W32], F32)
        NHB = 2
        hb_t = sb("hb", [128, NHB, D], BF16)
        NTB = 2
        tb_t = sb("tbuf", [128, NTB, D], F32)
        NWB = 10
        wbf_t = sb("workbf", [128, NWB, T], BF16)
        cp_t = sb("cpk", [128, CP], F32)
        ws_t = sb("wslots", [128, NSLOT, 1024], BF16)
        wkv_t = sb("wkvbf", [128, 8, 512], BF16)
        kT_t = sb("kT", [128, 2, MEM], BF16)
        vpad_t = sb("vpad", [128, 2, 4, 128], BF16)
        ident_t = sb("ident", [128, 128], BF16)
        ones_t = sb("onesAB", [128, 2, 128], BF16)
        bd_t = sb("bdbf", [128, 2, 128], BF16)
        hbg_t = sb("hbgate", [128, 32], F32)
        st_t = sb("stats", [128, 64], F32)
        H_t = sb("ffnH", [128, 48, 2], F32)
        cy_t = sb("ffncarry", [128, 48, 3], F32)
        ps_t = es.enter_context(nc.psum_tensor("psall", [128, 8, 512], F32))

        pe, act, dve, pool = nc.tensor, nc.scalar, nc.vector, nc.gpsimd

        def bank(i):
            return ps_t[:, i, :]

        def bank_bf(i):
            return ps_t[:, i, :].bitcast(BF16)

        def bank2(i):
            return ps_t[:, i:i + 2, :].rearrange("p a b -> p (a b)")

        def col(c, n=1):
            return cp_t[:, c:c + n]

        eps_ap = st_t[:, 60:61]
        zero_ap = st_t[:, 62:63]
        negln2_ap = st_t[:, 63:64]

        def emit_all(S, plan, wseq):
            xr_b = [Buf(f"xr{i}") for i in range(NX)]
            hT_b = [Buf("hT0"), Buf("hT1")]
            h2T_b = Buf("h2T")
            ff_b = [Buf(f"ff{i}") for i in range(NJ)]
            ycat_b = [Buf(f"yc{i}") for i in range(8)]
            mrg_b = [Buf(f"mg{i}") for i in range(8)]
            gates_b = [Buf(f"gt{i}") for i in range(NGATE)]
            w32_b = [Buf(f"w32_{i}") for i in range(NW32)]
            cv_b = [Buf(f"cv{i}") for i in range(4)]
            ub_b = [Buf(f"ub{i}") for i in range(2)]
            hb_b = [Buf(f"hb{i}") for i in range(NHB)]
            tb_b = [Buf(f"tb{i}") for i in range(NTB)]
            wbf_b = [Buf(f"wbf{i}") for i in range(NWB)]
            cp_b = Buf("cpk")
            cpg1_b = Buf("cpg1")
            ws_b = [Buf(f"ws{i}") for i in range(NSLOT)]
            wkv_b = Buf("wkv")
            kT_b = Buf("kT")
            vpad_b = Buf("vpad")
            cst_b = Buf("consts")
            st_b = Buf("stats")
            H_b = Buf("ffnH")
            cy_b = Buf("ffncarry")
            bank_b = [Buf(f"bank{i}") for i in range(8)]
            NST = 19
            dummy_b = Buf("tblwarm")
            sts_b = [Buf(f"st{i}") for i in range(NST)]

            gates_r = Ring(list(range(NGATE)))
            w32_r = Ring(list(range(NW32)))
            hb_r = Ring(list(range(NHB)))
            tb_r = Ring(list(range(NTB)))
            wbf_r = Ring(list(range(NWB)))
            xr_r = Ring(list(range(NX)))
            bank_r = Ring(list(range(8)))
            stat_r = Ring(list(range(NST)))
            xs = {}

            wstate = {"issued": 0, "consumed": 0}

            stored = set()
            wbc_b = [Buf(f"wbc{i}") for i in range(NCHUNK)]

            def issue_weights(upto):
                while wstate["issued"] < min(upto, len(wseq)):
                    n = wstate["issued"]
                    c = wseq[n]
                    s = n % NSLOT
                    if c not in stored:
                        stored.add(c)
                        S.dma("pool", ws_t[:, s, :], wf_d[c * 128:(c + 1) * 128, :], [], [ws_b[s]], ws_b[s])
                        S.dma("sp", wb_d[c * 128:(c + 1) * 128, :], ws_t[:, s, :], [ws_b[s]], [wbc_b[c]], ws_b[s])
                    else:
                        S.dma("sp", ws_t[:, s, :], wb_d[c * 128:(c + 1) * 128, :], [wbc_b[c]], [ws_b[s]], ws_b[s])
                    wstate["issued"] += 1

            def next_w(cid):
                n = wstate["consumed"]
                wstate["consumed"] += 1
                if plan:
                    wseq.append(cid)
                    return n % NSLOT
                assert wseq[n] == cid
                issue_weights(n + NSLOT - 2)
                return n % NSLOT

            def wchunk(s, k):
                return ws_t[:, s, k * 128:(k + 1) * 128]

            def mm_chunk(s, rhs_list, rhs_bufs, bi, n=T):
                def fn():
                    last = None
                    nk = len(rhs_list)
                    for idx, (k, rhs) in enumerate(rhs_list):
                        last = pe.matmul(ps_t[:, bi, 0:n], lhsT=wchunk(s, k), rhs=rhs, start=(idx == 0), stop=(idx == nk - 1))
                    return last
                S.op("pe", fn, [ws_b[s]] + rhs_bufs, [bank_b[bi]])

            def rstd_of(src_ap, src_bufs, sq_scale, exp_bias, junk_ap, junk_buf):
                si = stat_r.next()
                c = 3 * si
                sbuf = sts_b[si]
                S.op("act", lambda: act.activation(out=junk_ap, in_=src_ap, func=AF.Square, scale=sq_scale,
                                                   accum_out=st_t[:, c:c + 1]),
                     src_bufs, [junk_buf, sbuf])
                S.op("act", lambda: act.activation(out=st_t[:, c + 1:c + 2], in_=st_t[:, c:c + 1], func=AF.Ln, bias=eps_ap, scale=1.0),
                     [sbuf, st_b], [sbuf])
                S.op("act", lambda: act.activation(out=st_t[:, c + 2:c + 3], in_=st_t[:, c + 1:c + 2], func=AF.Exp, scale=-0.5, bias=exp_bias),
                     [sbuf, st_b], [sbuf])
                return st_t[:, c + 2:c + 3], sbuf

            def norm_A(src_ap, src_bufs, gcol):
                hi = hb_r.next()
                r, rb = rstd_of(src_ap, src_bufs, 1.0 / 32.0, zero_ap, hb_t[:, hi, :], hb_b[hi])
                if gcol is None:
                    S.op("dve", lambda: dve.tensor_scalar(out=hb_t[:, hi, :], in0=src_ap, scalar1=r, scalar2=None, op0=ALU.mult),
                         src_bufs + [rb], [hb_b[hi]])
                else:
                    S.op("dve", lambda: dve.scalar_tensor_tensor(out=hb_t[:, hi, :], in0=src_ap, scalar=r, in1=cp_t[:, gcol:gcol + D],
                                                                 op0=ALU.mult, op1=ALU.mult),
                         src_bufs + [rb, cpg1_b if gcol == C_G1 else cp_b], [hb_b[hi]])
                return hi

            def norm_B(hi, dst3, dst_b, c0, c1, scale_col=None, evac="act"):
                bi = bank_r.next()

                def tr():
                    last = None
                    for k in range(8):
                        last = pe.transpose(bank_bf(bi)[:, k * 128:(k + 1) * 128], hb_t[:, hi, k * 128:(k + 1) * 128], ident_t[:])
                    return last
                S.op("pe", tr, [hb_b[hi], cst_b], [bank_b[bi]])
                if scale_col is None and evac == "dve":
                    S.op("dve", lambda: dve.tensor_copy(out=dst3[:, :, c0:c1], in_=bank_bf(bi).rearrange("p (k t) -> p k t", k=8)),
                         [bank_b[bi]], [dst_b])
                elif scale_col is None:
                    S.op("act", lambda: act.copy(out=dst3[:, :, c0:c1], in_=bank_bf(bi).rearrange("p (k t) -> p k t", k=8)),
                         [bank_b[bi]], [dst_b])
                else:
                    def ev():
                        last = None
                        for k in range(8):
                            last = act.activation(out=dst3[:, k, c0:c1], in_=bank_bf(bi)[:, k * 128:(k + 1) * 128], func=AF.Copy,
                                                  scale=col(scale_col + k))
                        return last
                    S.op("act", ev, [bank_b[bi], cp_b], [dst_b])

            def gen_norm(blocks, gcol, dst3, dst_b, scale_col=None, evac="act"):
                pend = None
                for (src_ap, src_bufs, c0, c1) in blocks:
                    hi = norm_A(src_ap, src_bufs, gcol)
                    yield
                    if pend is not None:
                        norm_B(pend[0], dst3, dst_b, pend[1], pend[2], scale_col, evac)
                        yield
                    pend = (hi, c0, c1)
                norm_B(pend[0], dst3, dst_b, pend[1], pend[2], scale_col, evac)
                yield

            kv_loads = {}

            def emit_KV_loads(ti, queue="pool"):
                seq = ti // TPS
                blocks = []
                for mb in range(2):
                    tb = tb_r.next()
                    r0 = seq * MEM + mb * 128
                    S.dma(queue, tb_t[:, tb, :], mem_d[r0:r0 + 128, :], [], [tb_b[tb]], tb_b[tb])
                    blocks.append((tb_t[:, tb, :], [tb_b[tb]], mb * 128, (mb + 1) * 128))
                kv_loads[ti] = blocks

            xs[0] = [xr_r.next() for _ in range(4)]

            def x0_load(b):
                xi = xs[0][b]
                S.dma("sp", xring_t[:, xi, :], x_d[b * 128:(b + 1) * 128, :], [], [xr_b[xi]], xr_b[xi])
            x0_load(0)
            S.dma("sp", cp_t[:, 0:1024], cp_d[:, 0:1024], [], [cpg1_b], cpg1_b)
            for b in range(1, 4):
                x0_load(b)
            emit_KV_loads(0, "sp")
            S.dma("sp", cp_t[:, 1024:CP], cp_d[:, 1024:CP], [], [cp_b], cp_b)

            def mk_consts():
                idf = w32_t[:, 0, 0:128]
                pool.memset(idf, 0.0)
                pool.affine_select(out=idf, in_=idf, pattern=[[-1, 128]],
                                   compare_op=ALU.not_equal, fill=1.0, base=0, channel_multiplier=1)
                pool.memset(ones_t[:], 0.0)
                pool.memset(ones_t[:, 0, 0:64], 1.0)
                pool.memset(ones_t[:, 1, 64:128], 1.0)
                pool.memset(vpad_t[:], 0.0)
                return pool.tensor_copy(out=ident_t[:], in_=idf)
            S.op("pool", mk_consts, [], [cst_b, vpad_b, w32_b[0]])
            w32_r.next()

            def mk_stat_consts():
                dve.memset(st_t[:], 0.0)
                dve.memset(st_t[:, 60:61], EPS)
                return dve.memset(st_t[:, 63:64], -math.log(2.0))
            S.op("dve", mk_stat_consts, [], [st_b] + sts_b)
            def table_warm():
                S.op("act", lambda: act.activation(out=st_t[:, 57:58], in_=st_t[:, 60:61], func=AF.Ln), [st_b], [dummy_b])
            table_warm()

            def emit_late_consts():
                S.op("dve", lambda: dve.tensor_copy(out=bd_t[:], in_=cp_t[:, C_BD:C_BD + 256].rearrange("p (a b) -> p a b", a=2)),
                     [cp_b], [cst_b])
                S.op("dve", lambda: dve.tensor_scalar(out=hbg_t[:, 0:24], in0=col(C_BG, 24), scalar1=0.5, scalar2=None, op0=ALU.mult),
                     [cp_b], [cst_b])

            def emit_xload(ti, queue="pool"):
                xs[ti] = []
                for b in range(4):
                    xi = xr_r.next()
                    xs[ti].append(xi)
                    r0 = ti * T + b * 128
                    S.dma(queue, xring_t[:, xi, :], x_d[r0:r0 + 128, :], [], [xr_b[xi]], xr_b[xi])

            def gen_N1(ti):
                hbuf = ti % 2
                blocks = [(xring_t[:, xs[ti][b], :], [xr_b[xs[ti][b]]], b * 128, (b + 1) * 128) for b in range(4)]
                yield from gen_norm(blocks, C_G1, hT_t[:, hbuf], hT_b[hbuf])

            def gen_KV(ti, hbuf=None):
                seq = ti // TPS
                hbuf = ti % 2 if hbuf is None else hbuf
                memT = hT_t[:, hbuf]
                memT_b = hT_b[hbuf]
                if ti not in kv_loads:
                    emit_KV_loads(ti)
                blocks = kv_loads[ti]
                yield from gen_norm(blocks, None, memT, memT_b, scale_col=C_GM)
                for c in range(2):
                    bi = bank_r.next()

                    def kfn(c=c, bi=bi):
                        last = None
                        for k in range(8):
                            last = pe.matmul(ps_t[:, bi, 0:MEM], lhsT=wkv_t[:, k, c * 128:(c + 1) * 128], rhs=memT[:, k, 0:MEM],
                                             start=(k == 0), stop=(k == 7))
                        return last
                    S.op("pe", kfn, [wkv_b, memT_b], [bank_b[bi]])
                    S.op("act", lambda c=c, bi=bi: act.copy(out=kT_t[:, c, :], in_=ps_t[:, bi, 0:MEM]), [bank_b[bi]], [kT_b])
                yield
                for mb in range(2):
                    bi = bank_r.next()

                    def vfn(mb=mb, bi=bi):
                        last = None
                        for k in range(8):
                            last = pe.matmul(ps_t[:, bi, 0:256], lhsT=memT[:, k, mb * 128:(mb + 1) * 128], rhs=wkv_t[:, k, 256:512],
                                             start=(k == 0), stop=(k == 7))
                        return last
                    S.op("pe", vfn, [wkv_b, memT_b], [bank_b[bi]])

                    def vcp(mb=mb, bi=bi):
                        src = ps_t[:, bi, 0:256].rearrange("p (j e d) -> p j e d", j=2, e=2)
                        dst = vpad_t[:, mb, :, :].rearrange("p (j e) (f d) -> p j e f d", j=2, f=2)
                        dve.tensor_copy(out=dst[:, :, 0, 0, :], in_=src[:, :, 0, :])
                        return dve.tensor_copy(out=dst[:, :, 1, 1, :], in_=src[:, :, 1, :])
                    S.op("dve", vcp, [bank_b[bi]], [vpad_b])
                yield

            def gen_M1(ti):
                first = (ti % TPS == 0)
                hbuf = ti % 2
                hT_rhs = [(k, hT_t[:, hbuf, k, :]) for k in range(8)]
                hTb = hT_b[hbuf]
                for q in range(4):
                    if first:
                        S.op("dve", lambda q=q: dve.memset(cv_t[:, q, 0:2], 0.0), [], [cv_b[q]])
                    else:
                        S.op("dve", lambda q=q: dve.tensor_copy(out=cv_t[:, q, 0:2], in_=cv_t[:, q, 512:514]), [cv_b[q]], [cv_b[q]])
                for j in range(2):
                    if first:
                        S.op("dve", lambda j=j: dve.memset(ub_t[:, j, 0:16], 0.0), [], [ub_b[j]])
                    else:
                        S.op("dve", lambda j=j: dve.tensor_copy(out=ub_t[:, j, 0:16], in_=ub_t[:, j, 512:528]), [ub_b[j]], [ub_b[j]])
                csb = {}
                tconv = {}
                att = {}
                att_pts = {}
                pool_pending = {}
                for pos, cid in enumerate(ORDER_IN):
                    s = next_w(pos)
                    bi = bank_r.next()
                    mm_chunk(s, hT_rhs, [hTb], bi)
                    if 4 <= cid < 8:
                        q = cid - 4
                        wi = w32_r.next()
                        csb[q] = wi
                        S.op("act", lambda bi=bi, wi=wi: act.copy(out=w32_t[:, wi, 0:T], in_=bank(bi)), [bank_b[bi]], [w32_b[wi]])
                    elif 8 <= cid < 12:
                        q = cid - 8
                        wi = csb[q]
                        S.op("dve", lambda bi=bi, wi=wi, q=q: dve.tensor_tensor(out=cv_t[:, q, 2:514], in0=bank(bi), in1=w32_t[:, wi, 0:T], op=ALU.mult),
                             [bank_b[bi], w32_b[wi]], [cv_b[q]])
                        wt = w32_r.next()
                        tconv[q] = wt
                        S.op("act", lambda q=q, wt=wt: act.activation(out=w32_t[:, wt, 0:T], in_=cv_t[:, q, 0:T], func=AF.Copy, scale=col(C_CW + 3 * q)),
                             [cv_b[q], cp_b], [w32_b[wt]])
                        S.op("dve", lambda q=q, wt=wt: dve.scalar_tensor_tensor(out=w32_t[:, wt, 0:T], in0=cv_t[:, q, 1:T + 1], scalar=col(C_CW + 3 * q + 1),
                                                                                in1=w32_t[:, wt, 0:T], op0=ALU.mult, op1=ALU.add),
                             [cv_b[q], cp_b, w32_b[wt]], [w32_b[wt]])
                        S.op("dve", lambda q=q, wt=wt: dve.scalar_tensor_tensor(out=w32_t[:, wt, 0:T], in0=cv_t[:, q, 2:T + 2], scalar=col(C_CW + 3 * q + 2),
                                                                                in1=w32_t[:, wt, 0:T], op0=ALU.mult, op1=ALU.add),
                             [cv_b[q], cp_b, w32_b[wt]], [w32_b[wt]])
                    elif cid < 4:
                        q = cid
                        wt = tconv[q]
                        S.op("dve", lambda bi=bi, wt=wt, q=q: dve.tensor_tensor(out=ycat_t[:, q, :], in0=bank(bi), in1=w32_t[:, wt, 0:T], op=ALU.mult),
                             [bank_b[bi], w32_b[wt]], [ycat_b[q]])
                    elif cid < 14:
                        j = cid - 12
                        S.op("act", lambda bi=bi, j=j: act.copy(out=ub_t[:, j, 16:528], in_=bank(bi)), [bank_b[bi]], [ub_b[j]])
                        U = ub_t[:, j, :]
                        a = w32_r.next()
                        S.op("dve", lambda a=a, U=U: dve.tensor_tensor(out=w32_t[:, a, 1:528], in0=U[:, 1:528], in1=U[:, 0:527], op=ALU.add),
                             [ub_b[j]], [w32_b[a]])
                        lo = 1
                        cur = a
                        if j == 1:
                            for sh in (2, 4):
                                nb = w32_r.next()
                                S.op("dve", lambda cur=cur, nb=nb, sh=sh, lo=lo: dve.tensor_tensor(
                                    out=w32_t[:, nb, lo + sh:528], in0=w32_t[:, cur, lo + sh:528], in1=w32_t[:, cur, lo:528 - sh], op=ALU.add),
                                    [w32_b[cur]], [w32_b[nb]])
                                cur = nb
                                lo += sh
                        sh = 2 if j == 0 else 8
                        wg = w32_r.next()
                        S.op("dve", lambda cur=cur, wg=wg, sh=sh, lo=lo, j=j: dve.scalar_tensor_tensor(
                            out=w32_t[:, wg, lo + sh:528], in0=w32_t[:, cur, lo:528 - sh], scalar=col(C_BETA + j),
                            in1=w32_t[:, cur, lo + sh:528], op0=ALU.mult, op1=ALU.add),
                            [w32_b[cur], cp_b], [w32_b[wg]])
                        pb = wbf_r.next()
                        if first:
                            S.op("dve", lambda wg=wg, j=j: dve.tensor_tensor(out=w32_t[:, wg, 0:16], in0=w32_t[:, wg, 16:32],
                                                                              in1=cp_t[:, C_RC + 16 * j:C_RC + 16 * j + 16], op=ALU.mult),
                                 [w32_b[wg], cp_b], [w32_b[wg]])
                        S.op("dve", lambda wg=wg, pb=pb, j=j, U=U: dve.scalar_tensor_tensor(
                            out=wbf_t[:, pb, :], in0=w32_t[:, wg, 16:528], scalar=col(C_RCW + j), in1=U[:, 16:528],
                            op0=ALU.mult, op1=ALU.subtract),
                            [w32_b[wg], cp_b, ub_b[j]], [wbf_b[pb]])
                        if first:
                            S.op("dve", lambda wg=wg, pb=pb, U=U: dve.tensor_tensor(out=wbf_t[:, pb, 0:16], in0=w32_t[:, wg, 0:16], in1=U[:, 16:32],
                                                                                   op=ALU.subtract),
                                 [w32_b[wg], ub_b[j], wbf_b[pb]], [wbf_b[pb]])
                        def pool_fin(pb=pb, j=j):
                            b2 = bank_r.next()
                            S.op("pe", lambda: pe.matmul(bank(b2), lhsT=bd_t[:, j, :], rhs=wbf_t[:, pb, :], start=True, stop=True),
                                 [cst_b, wbf_b[pb]], [bank_b[b2]])
                            S.op("act", lambda: act.activation(out=ycat_t[:, 4 + j, :], in_=bank(b2), func=AF.Copy, scale=col(C_PS + j)),
                                 [bank_b[b2], cp_b], [ycat_b[4 + j]])
                        pool_pending[j] = pool_fin
                    else:
                        j = cid - 14
                        qb = wbf_r.next()
                        S.op("act", lambda bi=bi, qb=qb: act.copy(out=wbf_t[:, qb, :], in_=bank(bi)), [bank_b[bi]], [wbf_b[qb]])

                        def stage1(j=j, qb=qb):
                            pts = []
                            for e in range(2):
                                for mc in range(2):
                                    b3 = bank_r.next()
                                    S.op("pe", lambda b3=b3, e=e, mc=mc: pe.matmul(
                                        bank(b3), lhsT=kT_t[e * 64:(e + 1) * 64, j, mc * 128:(mc + 1) * 128],
                                        rhs=wbf_t[e * 64:(e + 1) * 64, qb, :], start=True, stop=True),
                                        [kT_b, wbf_b[qb]], [bank_b[b3]])
                                    pt = wbf_r.next()
                                    S.op("act", lambda b3=b3, pt=pt: act.activation(out=wbf_t[:, pt, :], in_=bank(b3), func=AF.Exp, scale=0.125),
                                         [bank_b[b3]], [wbf_b[pt]])
                                    pts.append((e, mc, pt))
                            return pts

                        def stage2(pts, j=j):
                            bpv = bank_r.next()
                            bdn = bank_r.next()

                            def pv():
                                last = None
                                for idx, (e, mc, pt) in enumerate(pts):
                                    last = pe.matmul(bank(bpv), lhsT=vpad_t[:, mc, 2 * j + e, :], rhs=wbf_t[:, pt, :], start=(idx == 0), stop=(idx == 3))
                                return last

                            def dn():
                                last = None
                                for idx, (e, mc, pt) in enumerate(pts):
                                    last = pe.matmul(bank(bdn), lhsT=ones_t[:, e, :], rhs=wbf_t[:, pt, :], start=(idx == 0), stop=(idx == 3))
                                return last
                            ptb = [wbf_b[p[2]] for p in pts]
                            S.op("pe", pv, [vpad_b] + ptb, [bank_b[bpv]])
                            S.op("pe", dn, [cst_b] + ptb, [bank_b[bdn]])
                            rd = w32_r.next()
                            S.op("dve", lambda: dve.reciprocal(out=w32_t[:, rd, 0:T], in_=bank(bdn)), [bank_b[bdn]], [w32_b[rd]])
                            S.op("dve", lambda: dve.tensor_tensor(out=ycat_t[:, 6 + j, :], in0=bank(bpv), in1=w32_t[:, rd, 0:T], op=ALU.mult),
                                 [bank_b[bpv], w32_b[rd]], [ycat_b[6 + j]])
                        att[j] = (stage1, stage2)
                    for (jj, st, at_pos) in ((0, 1, 3), (0, 2, 5), (1, 1, 6), (1, 2, 9)):
                        if pos == at_pos:
                            if st == 1:
                                att_pts[jj] = att[jj][0]()
                            else:
                                att[jj][1](att_pts[jj])
                    for (jj, at_pos) in ((0, 7), (1, 10)):
                        if pos == at_pos:
                            pool_pending[jj]()
                    yield

            def gen_G(ti):
                hbuf = ti % 2
                hT_rhs = [(k, hT_t[:, hbuf, k, :]) for k in range(8)]
                hTb = hT_b[hbuf]
                for j in range(8):
                    gts = []
                    for i in range(3):
                        s = next_w(N_IN + 4 * j + i)
                        bi = bank_r.next()
                        mm_chunk(s, hT_rhs, [hTb], bi)
                        gi = gates_r.next()
                        gts.append(gi)
                        S.op("act", lambda bi=bi, gi=gi, i=i, j=j: act.activation(out=gates_t[:, gi, :], in_=bank(bi), func=AF.Tanh,
                                                                                  bias=hbg_t[:, i * 8 + j:i * 8 + j + 1], scale=0.5),
                             [bank_b[bi], cst_b], [gates_b[gi]])
                    s = next_w(N_IN + 4 * j + 3)
                    pbanks = []
                    for i, ks in enumerate(([0, 1, 2, 3], [4, 5], [6, 7])):
                        bi = bank_r.next()
                        pbanks.append(bi)
                        mm_chunk(s, [(k, ycat_t[:, k, :]) for k in ks], [ycat_b[k] for k in ks], bi)
                    m0 = w32_r.next()
                    m1 = w32_r.next()
                    S.op("dve", lambda gi=gts[0], bi=pbanks[0], m0=m0: dve.scalar_tensor_tensor(
                        out=w32_t[:, m0, 0:T], in0=gates_t[:, gi, :], scalar=1.0, in1=bank(bi), op0=ALU.add, op1=ALU.mult),
                        [gates_b[gts[0]], bank_b[pbanks[0]]], [w32_b[m0]])
                    S.op("dve", lambda gi=gts[1], bi=pbanks[1], m1=m1: dve.scalar_tensor_tensor(
                        out=w32_t[:, m1, 0:T], in0=gates_t[:, gi, :], scalar=1.0, in1=bank(bi), op0=ALU.add, op1=ALU.mult),
                        [gates_b[gts[1]], bank_b[pbanks[1]]], [w32_b[m1]])
                    S.op("dve", lambda m0=m0, m1=m1: dve.tensor_tensor(out=w32_t[:, m0, 0:T], in0=w32_t[:, m0, 0:T], in1=w32_t[:, m1, 0:T], op=ALU.add),
                         [w32_b[m0], w32_b[m1]], [w32_b[m0]])
                    S.op("dve", lambda gi=gts[2], bi=pbanks[2], m1=m1: dve.scalar_tensor_tensor(
                        out=w32_t[:, m1, 0:T], in0=gates_t[:, gi, :], scalar=1.0, in1=bank(bi), op0=ALU.add, op1=ALU.mult),
                        [gates_b[gts[2]], bank_b[pbanks[2]]], [w32_b[m1]])
                    S.op("dve", lambda m0=m0, m1=m1, j=j: dve.tensor_tensor(out=mrg_t[:, j, :], in0=w32_t[:, m0, 0:T], in1=w32_t[:, m1, 0:T], op=ALU.add),
                         [w32_b[m0], w32_b[m1]], [mrg_b[j]])
                    yield
                table_warm()

            def acc_phase_emit(cid0, nk, lhs_fn, lhs_bufs_fn, KK=3):
                slots = {}

                def edge(ks, order=(0, 1, 2, 3)):
                    for k in ks:
                        slots[k] = next_w(cid0 + k)
                    for b in order:
                        def fn(b=b):
                            last = None
                            for k in ks:
                                for nh in range(2):
                                    last = pe.matmul(ps_t[:, 2 * b + nh, :], lhsT=lhs_fn(k, b), rhs=ws_t[:, slots[k], nh * 512:(nh + 1) * 512],
                                                     start=(k == 0), stop=(k == nk - 1))
                            return last
                        rb = [ws_b[slots[k]] for k in ks]
                        for k in ks:
                            rb = rb + lhs_bufs_fn(k)
                        S.op("pe", fn, rb, [bank_b[2 * b], bank_b[2 * b + 1]])
                recent = bank_r.hist[-8:]
                age = lambda b: max([len(recent) - 1 - recent[::-1].index(x) if x in recent else -1 for x in (2 * b, 2 * b + 1)])
                edge(list(range(KK)), order=sorted(range(4), key=age))
                for k in range(KK, nk - KK):
                    s = next_w(cid0 + k)

                    def fn(k=k, s=s):
                        last = None
                        for b in range(4):
                            for nh in range(2):
                                last = pe.matmul(ps_t[:, 2 * b + nh, :], lhsT=lhs_fn(k, b), rhs=ws_t[:, s, nh * 512:(nh + 1) * 512],
                                                 start=False, stop=False)
                        return last
                    S.op("pe", fn, [ws_b[s]] + lhs_bufs_fn(k), list(bank_b))
                edge(list(range(nk - KK, nk)))
                bank_r.i = 0

            def post_norm(ti, b, sq_scale, exp_bias, gcol):
                xi = xs[ti][b]
                o_ap = bank2(2 * b)
                tb = tb_r.next()
                r, rb = rstd_of(o_ap, [bank_b[2 * b], bank_b[2 * b + 1]], sq_scale, exp_bias, tb_t[:, tb, :], tb_b[tb])
                S.op("dve", lambda: dve.scalar_tensor_tensor(out=tb_t[:, tb, :], in0=o_ap, scalar=r,
                                                             in1=cp_t[:, gcol:gcol + D], op0=ALU.mult, op1=ALU.mult),
                     [bank_b[2 * b], bank_b[2 * b + 1], rb, cp_b], [tb_b[tb]])
                S.op("dve", lambda: dve.tensor_tensor(out=xring_t[:, xi, :], in0=xring_t[:, xi, :], in1=tb_t[:, tb, :], op=ALU.add),
                     [xr_b[xi], tb_b[tb]], [xr_b[xi]])

            def emit_O(ti):
                acc_phase_emit(N_IN + N_G, 8, lambda k, b: mrg_t[:, k, b * 128:(b + 1) * 128], lambda k: [mrg_b[k]])

            def emit_N2a(ti):
                for b in range(4):
                    post_norm(ti, b, 1.0 / 64.0, negln2_ap, C_G2)

            def gen_N2b(ti):
                blocks = [(xring_t[:, xs[ti][b], :], [xr_b[xs[ti][b]]], b * 128, (b + 1) * 128) for b in range(4)]
                yield from gen_norm(blocks, C_G3, h2T_t, h2T_b, evac=("dve" if ti == NT - 1 else "act"))

            def gen_F(ti):
                first = (ti % TPS == 0)
                h2_rhs = [(k, h2T_t[:, k, :]) for k in range(8)]
                bank_r.i = 4
                for j in range(NJ):
                    obuf = {}
                    for widx, (which, ch) in enumerate((("g", j), ("v", NJ + j))):
                        s = next_w(N_IN + N_G + N_O + 2 * j + widx)
                        bi = bank_r.next()
                        mm_chunk(s, h2_rhs, [h2T_b], bi)
                        ob = w32_r.next()
                        obuf[which] = ob
                        cw = C_FCW + 3 * ch
                        def ev(bi=bi, ob=ob, cw=cw, ch=ch, first=first):
                            if first:
                                act.activation(out=w32_t[:, ob, 0:T], in_=bank(bi), func=AF.Copy, scale=col(cw + 2))
                            else:
                                act.activation(out=w32_t[:, ob, 0:1], in_=ps_t[:, bi, 0:1], func=AF.Identity, bias=cy_t[:, ch, 0:1], scale=col(cw + 2))
                                act.activation(out=w32_t[:, ob, 1:2], in_=ps_t[:, bi, 1:2], func=AF.Identity, bias=cy_t[:, ch, 1:2], scale=col(cw + 2))
                                act.activation(out=w32_t[:, ob, 2:T], in_=ps_t[:, bi, 2:T], func=AF.Copy, scale=col(cw + 2))
                            return act.copy(out=H_t[:, ch, :], in_=ps_t[:, bi, T - 2:T])
                        S.op("act", ev, [bank_b[bi], cp_b] + ([] if first else [cy_b]), [w32_b[ob], H_b])
                        S.op("dve", lambda bi=bi, ob=ob, cw=cw: dve.scalar_tensor_tensor(out=w32_t[:, ob, 1:T], in0=ps_t[:, bi, 0:T - 1], scalar=col(cw + 1),
                                                                                         in1=w32_t[:, ob, 1:T], op0=ALU.mult, op1=ALU.add),
                             [bank_b[bi], cp_b, w32_b[ob]], [w32_b[ob]])
                        S.op("dve", lambda bi=bi, ob=ob, cw=cw: dve.scalar_tensor_tensor(out=w32_t[:, ob, 2:T], in0=ps_t[:, bi, 0:T - 2], scalar=col(cw),
                                                                                         in1=w32_t[:, ob, 2:T], op0=ALU.mult, op1=ALU.add),
                             [bank_b[bi], cp_b, w32_b[ob]], [w32_b[ob]])
                    og, ov = obuf["g"], obuf["v"]
                    S.op("act", lambda og=og: act.activation(out=w32_t[:, og, 0:T], in_=w32_t[:, og, 0:T], func=AF.Gelu_apprx_tanh),
                         [w32_b[og]], [w32_b[og]])
                    S.op("dve", lambda og=og, ov=ov, j=j: dve.tensor_tensor(out=ff_t[:, j, :], in0=w32_t[:, og, 0:T], in1=w32_t[:, ov, 0:T], op=ALU.mult),
                         [w32_b[og], w32_b[ov]], [ff_b[j]])
                    yield
                if ti % TPS != TPS - 1:
                    def carry1():
                        fw = cp_t[:, C_FCW:C_FCW + 132].rearrange("p (c k) -> p c k", k=3)
                        dve.tensor_tensor(out=cy_t[:, 0:N_UP, 0], in0=fw[:, :, 1], in1=H_t[:, 0:N_UP, 1], op=ALU.mult)
                        dve.tensor_tensor(out=cy_t[:, 0:N_UP, 2], in0=fw[:, :, 0], in1=H_t[:, 0:N_UP, 0], op=ALU.mult)
                        return dve.tensor_tensor(out=cy_t[:, 0:N_UP, 1], in0=fw[:, :, 0], in1=H_t[:, 0:N_UP, 1], op=ALU.mult)
                    S.op("dve", carry1, [H_b, cp_b, cy_b], [cy_b])
                table_warm()

            def emit_D(ti):
                acc_phase_emit(N_IN + N_G + N_O + N_UP, NJ, lambda k, b: ff_t[:, k, b * 128:(b + 1) * 128], lambda k: [ff_b[k]])

            def emit_Dpost(ti):
                need_carry = (ti % TPS != TPS - 1)
                for b in range(4):
                    post_norm(ti, b, 1.0 / 32.0, zero_ap, C_G4)
                    xi = xs[ti][b]
                    r0 = ti * T + b * 128
                    S.dma("pool", out_d[r0:r0 + 128, :], xring_t[:, xi, :], [xr_b[xi]], [], xr_b[xi])
                    if b == 0 and need_carry:
                        S.op("dve", lambda: dve.tensor_tensor(out=cy_t[:, 0:N_UP, 0], in0=cy_t[:, 0:N_UP, 0], in1=cy_t[:, 0:N_UP, 2], op=ALU.add),
                             [cy_b], [cy_b])

            issue_weights(2)
            S.dma("pool", wkv_t[:].rearrange("p a b -> p (a b)"), wkv_d, [], [wkv_b], wkv_b)
            run_interleaved(gen_N1(0))
            run_interleaved(gen_KV(0, hbuf=1))
            issue_weights(NSLOT - 2)
            emit_late_consts()
            if NT > 1:
                emit_xload(1, "sp")
            run_interleaved(gen_M1(0))
            if NT > 1:
                run_interleaved(gen_G(0), gen_N1(1), after={1, 2, 3, 4, 5, 6, 7})
            else:
                run_interleaved(gen_G(0))
            for i in range(NT):
                emit_O(i)
                emit_N2a(i)
                if i + 2 < NT and (i + 2) % TPS == 0:
                    emit_KV_loads(i + 2)
                kv = gen_KV(i + 2) if (i + 2 < NT and (i + 2) % TPS == 0) else None
                if i + 1 < NT:
                    def side(i=i, kv=kv):
                        yield from gen_N2b(i)
                        if kv is not None:
                            next(kv)
                            yield
                            next(kv)
                            yield
                    run_interleaved(gen_M1(i + 1), side(), after={0, 1, 4, 5, 7, 9, 11, 13, 14, 15})
                else:
                    run_interleaved(gen_N2b(i))
                if kv is not None:
                    run_interleaved(kv)
                run_interleaved(gen_F(i))
                emit_D(i)
                emit_Dpost(i)
                if i + 2 < NT:
                    emit_xload(i + 2)
                if i + 1 < NT:
                    run_interleaved(gen_G(i + 1), gen_N1(i + 2) if i + 2 < NT else None, after={1, 2, 3, 4, 5, 6, 7})
            return xr_b

        wseq = []
        emit_all(PlanSched(), True, wseq)
        S = Sched(nc, es)
        xr_b = emit_all(S, False, wseq)
        S.finalize([("pool", b) for b in xr_b])
    return nc


_NC_CACHE = {}


def _type_a(W):
    K, C = W.shape
    assert K == 1024
    return W.reshape(8, 128, C // 128, 128).transpose(2, 1, 0, 3).reshape(C // 128, 128, 1024)


def kernel(x, mem, g_mix_pre, w_in, conv_w, pool_w, pool_scale, g_mem, w_kv,
           w_br_conv, w_br_pool, w_br_attn, w_gate, b_gate, w_o, g_mix_post,
           g_ffn_pre, w_up, ffn_conv_w, w_down, g_ffn_post):
    f = lambda a: np.asarray(a, dtype=np.float32)
    x, mem = f(x), f(mem)
    w_in, w_gate, w_o, w_up, w_down, w_kv = f(w_in)[0], f(w_gate)[0], f(w_o)[0], f(w_up)[0], f(w_down)[0], f(w_kv)[0]
    wbr = np.concatenate([f(w_br_conv)[0], f(w_br_pool)[0], f(w_br_attn)[0]], axis=0)

    a_in = _type_a(w_in)
    a_gate = _type_a(w_gate)
    a_br = _type_a(wbr)
    a_up = _type_a(w_up)
    chunks = [a_in[c] for c in ORDER_IN]
    for j in range(8):
        for i in range(3):
            chunks.append(a_gate[i * 8 + j])
        chunks.append(a_br[j])
    for k in range(8):
        chunks.append(w_o[k * 128:(k + 1) * 128, :])
    for j in range(NJ):
        chunks.append(a_up[j])
        chunks.append(a_up[NJ + j])
    for k in range(NJ):
        chunks.append(w_down[k * 128:(k + 1) * 128, :])
    wf = np.ascontiguousarray(np.stack(chunks, axis=0).reshape(NCHUNK * 128, 1024))
    wkv = np.ascontiguousarray(w_kv.reshape(8, 128, 512).transpose(1, 0, 2).reshape(128, 8 * 512))

    cp = np.zeros((128, CP), np.float32)
    cp[:, C_G1:C_G1 + D] = f(g_mix_pre)[0][None, :]
    cp[:, C_G2:C_G2 + D] = f(g_mix_post)[0][None, :]
    cp[:, C_G3:C_G3 + D] = f(g_ffn_pre)[0][None, :]
    cp[:, C_G4:C_G4 + D] = f(g_ffn_post)[0][None, :]
    cw = f(conv_w)[0]
    cp[:, C_CW:C_CW + 12] = cw.reshape(3, 4, 128).transpose(2, 1, 0).reshape(128, 12)
    fcw = f(ffn_conv_w)[0]
    cp[:, C_FCW:C_FCW + 132] = fcw.reshape(3, 44, 128).transpose(2, 1, 0).reshape(128, 132)
    cp[:, C_BG:C_BG + 24] = f(b_gate)[0].reshape(24, 128).T
    cp[:, C_PS:C_PS + 2] = f(pool_scale)[0].reshape(2, 128).T
    cp[:, C_GM:C_GM + 8] = f(g_mem)[0].reshape(8, 128).T
    wins = np.array([[2, 4], [8, 16]], np.float32)
    for j in range(2):
        cp[0:64, C_BETA + j] = 0.0
        cp[64:128, C_BETA + j] = 1.0
        cp[0:64, C_RCW + j] = 1.0 / wins[j, 0]
        cp[64:128, C_RCW + j] = 1.0 / wins[j, 1]
        t = np.arange(16, dtype=np.float32) + 1.0
        cp[0:64, C_RC + 16 * j:C_RC + 16 * j + 16] = 1.0 / np.minimum(wins[j, 0], t)
        cp[64:128, C_RC + 16 * j:C_RC + 16 * j + 16] = 1.0 / np.minimum(wins[j, 1], t)
    pw = f(pool_w)[0]
    bd = np.zeros((128, 2, 128), np.float32)
    for g in range(4):
        j, e = g // 2, g % 2
        bd[e * 64:(e + 1) * 64, j, e * 64:(e + 1) * 64] = pw[g]
    cp[:, C_BD:C_BD + 256] = bd.reshape(128, 256)

    if "nc" not in _NC_CACHE:
        _NC_CACHE["nc"] = build_program()
    nc = _NC_CACHE["nc"]
    in_maps = []
    for c in range(NCORES):
        in_maps.append({
            "x": np.ascontiguousarray(x[c * BPC:(c + 1) * BPC].reshape(TOK, D)),
            "mem": np.ascontiguousarray(mem[c * BPC:(c + 1) * BPC].reshape(BPC * MEM, D)),
            "wf": wf, "wkv": wkv, "cpack": cp,
        })
    res = run_bass_kernel_spmd(nc, in_maps, core_ids=list(range(NCORES)))
    out = np.stack([np.asarray(r["out"]).reshape(BPC, SEQ, D) for r in res.results], axis=0)
    return out.reshape(NCORES * BPC, SEQ, D).astype(np.float32)
```
